# Optimizing a Trainium2 kernel written in Bass

```python
import jax, jax.numpy as jnp
from jax import lax
import numpy as np

D_MODEL = 1024
BATCH = 8
SEQ = 2048
DEPTH = 2

EPS = 1e-6
N_EVEN = (DEPTH + 1) // 2
N_ODD = DEPTH // 2

A_HEADS = 4
A_KDIM = 128
A_VDIM = 128
A_FDIM = A_HEADS * A_KDIM
A_WIDTH = A_HEADS * A_VDIM
CHUNK = 32
B_GROUPS = 4
B_GDIM = 128
B_WIDTH = B_GROUPS * B_GDIM
AB_SPLITS = (A_FDIM, 2 * A_FDIM, 3 * A_FDIM, 3 * A_FDIM + A_WIDTH, 3 * A_FDIM + 2 * A_WIDTH)
AB_IN = 3 * A_FDIM + 2 * A_WIDTH + B_WIDTH
AB_OUT = A_WIDTH + B_WIDTH
C_HEADS = 16
C_KV = 4
C_GROUP = C_HEADS // C_KV
C_HDIM = 64
C_QKV = (C_HEADS + 2 * C_KV) * C_HDIM
WINDOW = 128
QBLOCK = 128
D_FF = -(-8 * D_MODEL // (3 * 256)) * 256

kernel_name = "hybrid_hgrn2_fnet_swa_encoder"


def _rmsnorm(x, gain):
    xf = x.astype(jnp.float32)
    y = xf * lax.rsqrt(jnp.mean(xf * xf, axis=-1, keepdims=True) + EPS)
    return (y * gain.astype(jnp.float32)).astype(x.dtype)


def _gla_chunkwise(q, k, v, logf):
    b, h, t, dk = q.shape
    dv = v.shape[-1]
    n = t // CHUNK
    q, k, v, logf = (a.reshape(b, h, n, CHUNK, a.shape[-1]) for a in (q, k, v, logf))
    g = jnp.cumsum(logf, axis=3)
    g_ref = g[:, :, :, CHUNK // 2 - 1:CHUNK // 2, :]
    g_last = g[:, :, :, -1:, :]
    qr = q * jnp.exp(g - g_ref)
    kr = k * jnp.exp(g_ref - g)
    scores = jnp.einsum('bhnck,bhnsk->bhncs', qr, kr)
    causal_in_chunk = jnp.tril(jnp.ones((CHUNK, CHUNK), dtype=bool))
    scores = jnp.where(causal_in_chunk, scores, 0.0)
    o_intra = jnp.einsum('bhncs,bhnsv->bhncv', scores, v)
    q_in = q * jnp.exp(g)
    k_out = k * jnp.exp(g_last - g)
    decay = jnp.exp(g_last[:, :, :, 0, :])

    def step(state, xs):
        q_c, k_c, v_c, d_c = xs
        o_c = jnp.einsum('bhck,bhkv->bhcv', q_c, state)
        state = d_c[..., None] * state + jnp.einsum('bhck,bhcv->bhkv', k_c, v_c)
        return state, o_c

    xs = tuple(jnp.moveaxis(a, 2, 0) for a in (q_in, k_out, v, decay))
    s0 = jnp.zeros((b, h, dk, dv), jnp.float32)
    _, o_inter = lax.scan(step, s0, xs)
    o = o_intra + jnp.moveaxis(o_inter, 0, 2)
    return o.reshape(b, h, t, dv)


def _hgrn2_fourier_mixer(hn, w_in, lb, out_gain, w_out):
    b, t, _ = hn.shape
    proj = hn @ w_in
    q, zf, zb, iv, gate, u = jnp.split(proj, AB_SPLITS, axis=-1)

    def heads(a, d):
        return a.reshape(b, t, A_HEADS, d).transpose(0, 2, 1, 3).astype(jnp.float32)

    qh = heads(q, A_KDIM) * (A_KDIM ** -0.5)
    vh = heads(iv, A_VDIM)

    def gates(z, lb_dir):
        lb_h = lb_dir.reshape(A_HEADS, 1, A_KDIM)
        f = lb_h + (1.0 - lb_h) * jax.nn.sigmoid(heads(z, A_KDIM))
        return 1.0 - f, jnp.log(f)

    k_f, lf_f = gates(zf, lb[0])
    k_b, lf_b = gates(zb, lb[1])
    o_fwd = _gla_chunkwise(qh, k_f, vh, lf_f)
    flip = lambda a: jnp.flip(a, axis=2)
    o_bwd = flip(_gla_chunkwise(flip(qh), flip(k_b), flip(vh), flip(lf_b)))
    o = o_fwd + o_bwd
    o = o * lax.rsqrt(jnp.mean(o * o, axis=-1, keepdims=True) + EPS) * out_gain.astype(jnp.float32)[:, None, :]
    o = o.transpose(0, 2, 1, 3).reshape(b, t, A_WIDTH).astype(hn.dtype) * jax.nn.silu(gate)
    ug = u.reshape(b, t, B_GROUPS, B_GDIM).astype(jnp.float32)
    fo = jnp.fft.fft2(ug, axes=(1, 3), norm='ortho').real.reshape(b, t, B_WIDTH).astype(hn.dtype)
    return jnp.concatenate([o, fo], axis=-1) @ w_out


def _window_attention_mixer(hn, w_qkv, sink, w_out):
    b, t, _ = hn.shape
    qkv = hn @ w_qkv
    q, k, v = jnp.split(qkv, (C_HEADS * C_HDIM, (C_HEADS + C_KV) * C_HDIM), axis=-1)
    q = q.reshape(b, t, C_KV, C_GROUP, C_HDIM)
    k = k.reshape(b, t, C_KV, C_HDIM)
    v = v.reshape(b, t, C_KV, C_HDIM)
    pad = ((0, 0), (WINDOW, WINDOW), (0, 0), (0, 0))
    kp = jnp.pad(k, pad)
    vp = jnp.pad(v, pad)
    nb = t // QBLOCK
    span = QBLOCK + 2 * WINDOW
    qb = jnp.moveaxis(q.reshape(b, nb, QBLOCK, C_KV, C_GROUP, C_HDIM), 1, 0)
    slopes = (2.0 ** (-8.0 * jnp.arange(1, C_HEADS + 1, dtype=jnp.float32) / C_HEADS)).reshape(C_KV, C_GROUP, 1, 1)
    sink_logit = sink.astype(jnp.float32).reshape(C_KV, C_GROUP, 1, 1)
    scale = C_HDIM ** -0.5

    def block(args):
        j, q_blk = args
        start = j * QBLOCK
        k_blk = lax.dynamic_slice_in_dim(kp, start, span, axis=1)
        v_blk = lax.dynamic_slice_in_dim(vp, start, span, axis=1)
        qpos = start + jnp.arange(QBLOCK)
        kpos = start - WINDOW + jnp.arange(span)
        dist = jnp.abs(qpos[:, None] - kpos[None, :])
        valid = (dist <= WINDOW) & (kpos >= 0)[None, :] & (kpos < t)[None, :]
        s = jnp.einsum('bqkgd,bskd->bkgqs', q_blk, k_blk).astype(jnp.float32) * scale
        s = jnp.where(valid, s - slopes * dist.astype(jnp.float32), -jnp.inf)
        s = jnp.concatenate([s, jnp.broadcast_to(sink_logit, s.shape[:-1] + (1,))], axis=-1)
        p = jax.nn.softmax(s, axis=-1)[..., :-1]
        return jnp.einsum('bkgqs,bskd->bqkgd', p.astype(v_blk.dtype), v_blk)

    o = lax.map(block, (jnp.arange(nb), qb))
    o = jnp.moveaxis(o, 0, 1).reshape(b, t, C_HEADS * C_HDIM)
    return o @ w_out


def _swiglu(hn, w1, w3, w2):
    return (jax.nn.silu(hn @ w1) * (hn @ w3)) @ w2


def setup_inputs(seed: int = 0) -> dict:
    key = jax.random.key(seed)
    ks = jax.random.split(key, 13)
    f32 = jnp.float32
    nrm = lambda k, shape, fan_in: jax.random.normal(k, shape, f32) * (fan_in ** -0.5)
    return {
        "x": jax.random.normal(ks[0], (BATCH, SEQ, D_MODEL), f32),
        "norm_gains": 1.0 + 0.05 * jax.random.normal(ks[1], (DEPTH, 4, D_MODEL), f32),
        "ab_w_in": nrm(ks[2], (N_EVEN, D_MODEL, AB_IN), D_MODEL),
        "ab_lb_table": 0.5 * jax.random.normal(ks[3], (N_EVEN + 1, 2, A_FDIM), f32),
        "ab_out_gain": 1.0 + 0.05 * jax.random.normal(ks[4], (N_EVEN, A_HEADS, A_VDIM), f32),
        "ab_w_out": nrm(ks[5], (N_EVEN, AB_OUT, D_MODEL), AB_OUT),
        "c_w_qkv": nrm(ks[6], (N_ODD, D_MODEL, C_QKV), D_MODEL),
        "c_sink": 0.5 * jax.random.normal(ks[7], (N_ODD, C_HEADS), f32),
        "c_w_out": nrm(ks[8], (N_ODD, C_HEADS * C_HDIM, D_MODEL), C_HEADS * C_HDIM),
        "ffn_w1": nrm(ks[9], (DEPTH, D_MODEL, D_FF), D_MODEL),
        "ffn_w3": nrm(ks[10], (DEPTH, D_MODEL, D_FF), D_MODEL),
        "ffn_w2": nrm(ks[11], (DEPTH, D_FF, D_MODEL), D_FF),
    }


def reference(x, norm_gains, ab_w_in, ab_lb_table, ab_out_gain, ab_w_out, c_w_qkv, c_sink, c_w_out, ffn_w1, ffn_w3, ffn_w2):
    lb_all = jnp.cumsum(jax.nn.softmax(ab_lb_table.astype(jnp.float32), axis=0), axis=0)
    for layer in range(DEPTH):
        gains = norm_gains[layer]
        hn = _rmsnorm(x, gains[0])
        if layer % 2 == 0:
            e = layer // 2
            m = _hgrn2_fourier_mixer(hn, ab_w_in[e], lb_all[e], ab_out_gain[e], ab_w_out[e])
        else:
            o = layer // 2
            m = _window_attention_mixer(hn, c_w_qkv[o], c_sink[o], c_w_out[o])
        x = x + _rmsnorm(m, gains[1])
        hn = _rmsnorm(x, gains[2])
        x = x + _rmsnorm(_swiglu(hn, ffn_w1[layer], ffn_w3[layer], ffn_w2[layer]), gains[3])
    return x
```

```python
import math
from contextlib import ExitStack

import numpy as np
import ml_dtypes
import concourse.bass as bass
import concourse.mybir as mybir
from concourse.bass_utils import run_bass_kernel_spmd

F32 = mybir.dt.float32
BF16 = mybir.dt.bfloat16
AF = mybir.ActivationFunctionType
ALU = mybir.AluOpType

T = 2048
D = 1024
NT = 16
DFF = 2816
NFC = 22
EPS = 1e-6

ENGS = ("pe", "act", "dve", "pool", "sp")
NDMA_SLOTS = 32


class Op:
    __slots__ = ("eng", "fn", "deps", "is_dma", "slot", "dval", "needed", "sval", "idx")

    def __init__(self, eng, fn, is_dma):
        self.eng = eng
        self.fn = fn
        self.deps = []
        self.is_dma = is_dma
        self.slot = None
        self.dval = None
        self.needed = False
        self.sval = None


class Sched:
    def __init__(self):
        self.ops = {e: [] for e in ENGS}
        self.last_w = {}
        self.readers = {}
        self.ndma = 0
        self.slot_last = [None] * NDMA_SLOTS
        self.slot_cnt = [0] * NDMA_SLOTS
        self.bar = []
        self.pool_dmas = []
        self.nq = [0, 0]

    def barrier(self):
        b = []
        for e in ENGS:
            for op in reversed(self.ops[e]):
                if not op.is_dma:
                    b.append(op)
                    break
        for s in range(NDMA_SLOTS):
            if self.slot_last[s] is not None:
                b.append(self.slot_last[s])
        self.bar = b

    def _add(self, eng, fn, reads, writes, is_dma, nobar=False):
        op = Op(eng, fn, is_dma)
        deps = []
        for r in reads:
            w = self.last_w.get(r)
            if w is not None:
                deps.append(w)
        for w_ in writes:
            w = self.last_w.get(w_)
            if w is not None:
                deps.append(w)
            deps.extend(self.readers.get(w_, ()))
        if not nobar:
            deps.extend(self.bar)
        seen = set()
        latest = {}
        for d in deps:
            if d is op or id(d) in seen:
                continue
            seen.add(id(d))
            if d.is_dma:
                op.deps.append(d)
                continue
            if (not is_dma) and d.eng == "pe" and eng == "pe":
                continue
            cur = latest.get(d.eng)
            if cur is None or d.idx > cur.idx:
                latest[d.eng] = d
        for d in latest.values():
            op.deps.append(d)
            d.needed = True
        for r in reads:
            self.readers.setdefault(r, []).append(op)
        for w_ in writes:
            self.last_w[w_] = op
            self.readers[w_] = []
        if is_dma and eng == "pool":
            self.pool_dmas.append(op)
            if len(self.pool_dmas) > 4:
                pd = self.pool_dmas[-5]
                if id(pd) not in seen:
                    seen.add(id(pd))
                    op.deps.append(pd)
        if is_dma:
            half = NDMA_SLOTS // 2
            qi = 1 if eng == "pool" else 0
            s = qi * half + self.nq[qi] % half
            self.nq[qi] += 1
            self.ndma += 1
            prev = self.slot_last[s]
            if prev is not None and id(prev) not in seen:
                op.deps.append(prev)
            self.slot_cnt[s] += 16
            op.slot = s
            op.dval = self.slot_cnt[s]
            self.slot_last[s] = op
        op.idx = len(self.ops[eng])
        self.ops[eng].append(op)
        return op

    def op(self, eng, fn, reads=(), writes=()):
        return self._add(eng, fn, reads, writes, False)

    def dma(self, eng, fn, reads=(), writes=(), nobar=False):
        return self._add(eng, fn, reads, writes, True, nobar)

    def emit(self, nc):
        with ExitStack() as es:
            esem = {e: es.enter_context(nc.semaphore("s_" + e)) for e in ENGS if e != "sp"}
            dsem = [es.enter_context(nc.semaphore("d%d" % i)) for i in range(NDMA_SLOTS)]
            for e in ENGS:
                c = 0
                for op in self.ops[e]:
                    if op.is_dma:
                        continue
                    if op.needed:
                        c += 1
                        op.sval = c
            self.stats = {e: (len(self.ops[e]), sum(1 for o in self.ops[e] if o.needed and not o.is_dma)) for e in ENGS}
            block = es.enter_context(nc.Block())
            hw = {"pe": block.tensor, "act": block.scalar, "dve": block.vector,
                  "pool": block.gpsimd, "sp": block.sync}

            def run(ename):
                ops = self.ops[ename]

                def body(eng):
                    known = {}
                    for op in ops:
                        for d in op.deps:
                            if d.is_dma:
                                key = ("d", d.slot)
                                sem = dsem[d.slot]
                                val = d.dval
                            else:
                                key = ("e", d.eng)
                                sem = esem[d.eng]
                                val = d.sval
                            if known.get(key, 0) >= val:
                                continue
                            eng.wait_ge(sem, val)
                            known[key] = val
                        if callable(op.fn):
                            ins = op.fn(eng)
                        else:
                            ins = getattr(eng, op.fn[0])(*op.fn[1], **op.fn[2])
                        if op.is_dma:
                            ins.then_inc(dsem[op.slot], 16)
                        elif op.needed:
                            ins.then_inc(esem[ename], 1)
                    if ename == "sp":
                        for s in range(NDMA_SLOTS):
                            if self.slot_cnt[s] > 0:
                                eng.wait_ge(dsem[s], self.slot_cnt[s])

                hw[ename](body)

            for e in ENGS:
                run(e)


def I(name, *args, **kw):
    return (name, args, kw)


def mkap(base, dims):
    return bass.AP(tensor=base.tensor, offset=base.offset,
                   ap=[list(base.ap[0])] + [list(d) for d in dims])


_CONSTS = None


def _consts():
    global _CONSTS
    if _CONSTS is not None:
        return _CONSTS
    bf = ml_dtypes.bfloat16
    c = {}
    c["ident"] = np.eye(128, dtype=np.float32).astype(bf)
    c["ident32"] = np.eye(128, dtype=np.float32)
    t = np.arange(T, dtype=np.int64)
    ph = (np.outer(t, t) % T).astype(np.float64) * (2.0 * np.pi / T)
    tab = np.stack([np.cos(ph), np.sin(ph)], axis=1) / math.sqrt(T)
    c["dft_t"] = np.ascontiguousarray(tab.astype(np.float32).astype(bf))
    cc = np.arange(128, dtype=np.int64)
    phc = (np.outer(cc, cc) % 128).astype(np.float64) * (2.0 * np.pi / 128)
    ccsc = np.concatenate([np.cos(phc), -np.sin(phc)], axis=1) / math.sqrt(128.0)
    alt = np.where(np.arange(128) % 2 == 0, 1.0, -1.0)[:, None] / math.sqrt(T)
    ccsc = np.concatenate([ccsc, alt, np.zeros((128, 1))], axis=1)
    c["dft_c"] = ccsc.astype(np.float32).astype(bf)
    s = np.arange(128)[:, None]
    q = np.arange(128)[None, :]
    mf = (s <= q).astype(np.float32)
    mb = (s >= q).astype(np.float32)
    c["hmask"] = np.concatenate([mf, mb], axis=1).astype(np.float32).astype(bf)
    j = np.arange(512)
    rm = np.stack([(j % 32 != 0), (j % 128 != 0)], axis=0).astype(np.float32)
    c["rmask"] = np.ascontiguousarray(np.broadcast_to(rm[None], (128, 2, 512))).astype(np.float32).astype(bf)
    slopes = 2.0 ** (-8.0 * np.arange(1, 17, dtype=np.float64) / 16.0)
    scale = 64.0 ** -0.5
    bias = np.zeros((128, 3, 16, 128), dtype=np.float64)
    for oi, off in enumerate((-1, 0, 1)):
        dist = np.abs(q - (s + 128 * off)).astype(np.float64)
        valid = dist <= 128
        for h in range(16):
            b = -slopes[h] * dist / scale
            bias[:, oi, h, :] = np.where(valid, b, -1.0e9)
    c["abias"] = bias.astype(np.float32)
    _CONSTS = c
    return c


def build(stage=99, dbg=False):
    nc = bass.Bass("TRN2", target_bir_lowering=False)

    def din(name, shape, dt=F32):
        return nc.dram_tensor(name, list(shape), dt, kind="ExternalInput").ap()

    x_d = din("x", [T, D])
    gains_d = din("norm_gains", [2, 4, D])
    w_in_d = din("ab_w_in", [D, 3072])
    lb_d = din("ab_lb_table", [2, 2, 512])
    og_d = din("ab_out_gain", [4, 128])
    w_abo_d = din("ab_w_out", [D, D])
    w_qkv_d = din("c_w_qkv", [D, 1536])
    sink_d = din("c_sink", [1, 16])
    w_co_d = din("c_w_out", [D, D])
    w1_d = din("ffn_w1", [2, D, DFF])
    w3_d = din("ffn_w3", [2, D, DFF])
    w2_d = din("ffn_w2", [2, DFF, D])
    ident_d = din("k_ident", [128, 128], BF16)
    ident32_d = din("k_ident32", [128, 128])
    dft_t_d = din("k_dft_t", [T, 2, T], BF16)
    dft_c_d = din("k_dft_c", [128, 258], BF16)
    hmask_d = din("k_hmask", [128, 256], BF16)
    rmask_d = din("k_rmask", [128, 2, 512], BF16)
    abias_d = din("k_abias", [128, 3, 16, 128])
    y_d = nc.dram_tensor("y", [T, D], F32, kind="ExternalOutput").ap()

    S = Sched()
    with ExitStack() as es:
        ARENA_B = 212800
        arena = es.enter_context(nc.sbuf_tensor("arena", [128, ARENA_B // 4], F32))
        pp = [es.enter_context(nc.psum_tensor("pp%d" % i, [128, 1024], F32)) for i in range(4)]

        def V(off, dt, shape):
            n = 1
            for s_ in shape:
                n *= s_
            assert off % 4 == 0
            if dt == F32:
                ap = arena[:, off // 4: off // 4 + n]
            else:
                assert n % 2 == 0
                ap = arena[:, off // 4: off // 4 + n // 2].bitcast(BF16)
            if len(shape) == 2:
                ap = ap.rearrange("p (a b) -> p a b", a=shape[0])
            elif len(shape) == 3:
                ap = ap.rearrange("p (a b c) -> p a b c", a=shape[0], b=shape[1])
            return ap

        def bank(i):
            return pp[i // 2][:, (i % 2) * 512:(i % 2) * 512 + 512]

        def bankb(i):
            return pp[i // 2][:, (i % 2) * 512:(i % 2) * 512 + 512].bitcast(BF16)

        bank_ctr = [0]
        pair_ctr = [0]

        def nbank():
            b = bank_ctr[0] % 8
            bank_ctr[0] += 1
            return b

        X0 = 0
        C0 = 65536
        HN0 = 73728
        WR0 = 106496
        PH0 = 139264
        X = V(X0, F32, [NT, D])
        HN = V(HN0, BF16, [8, T])
        WR = V(WR0, BF16, [16, 1024])
        co = [C0]

        def calloc(dt, shape):
            n = 1
            for s_ in shape:
                n *= s_
            nb = n * (4 if dt == F32 else 2)
            nb = (nb + 3) // 4 * 4
            ap = V(co[0], dt, shape)
            co[0] += nb
            assert co[0] <= HN0
            return ap

        IDENT = calloc(BF16, [128])
        ONES = calloc(BF16, [128])
        DFTC = calloc(BF16, [258])
        HMASK = calloc(BF16, [256])
        RMASK = calloc(BF16, [2, 512])
        CST = calloc(F32, [52])
        GPRE = CST[:, 0:32].rearrange("p (a b) -> p a b", a=4)
        LBT = CST[:, 32:48].rearrange("p (a b c) -> p a b c", a=2, b=2)
        LB = calloc(F32, [2, 4])
        OML = calloc(F32, [2, 4])
        OGAIN = CST[:, 48:52]
        ESINK = calloc(F32, [16])
        EPSC = calloc(F32, [1])
        ONEC = calloc(F32, [1])
        LNOML = calloc(F32, [2, 4])
        SS = calloc(F32, [16])
        LNV = calloc(F32, [16])
        RSTD = calloc(F32, [16])
        SS2 = calloc(F32, [4])
        S32 = calloc(F32, [4, 128])
        SBF = calloc(BF16, [4, 128])
        DEN = calloc(F32, [8])
        DEN3 = calloc(F32, [12])

        def wr_cols(slot):
            return WR[:, slot, :].rearrange("p (c n) -> p c n", c=8)

        def load_colblock(w2d, col0, slot, ncols=128):
            src = w2d[:, col0:col0 + ncols].rearrange("(c p) n -> p c n", p=128)
            dst = WR[:, slot, 0:8 * ncols].rearrange("p (c n) -> p c n", c=8)
            S.dma("pool", I("dma_start", out=dst, in_=src), writes=[("wr", slot)], nobar=True)

        def load_rowblock(w2d, row0, slot):
            src = w2d[row0:row0 + 128, :]
            S.dma("pool", I("dma_start", out=WR[:, slot, :], in_=src), writes=[("wr", slot)], nobar=True)

        def dump(name, ap, dt, shape):
            S.barrier()
            d = nc.dram_tensor("dbg_" + name, [128] + list(shape), dt, kind="ExternalOutput").ap()
            S.dma("sp", I("dma_start", out=d, in_=ap))

        def fin():
            S.barrier()
            S.emit(nc)
            return nc

        for i in range(NT):
            S.dma("sp", I("dma_start", out=X[:, i, :], in_=x_d[i * 128:(i + 1) * 128, :]),
                  writes=[("X", i)])
        S.dma("sp", I("dma_start", out=IDENT, in_=ident_d), writes=["IDENT"])
        S.dma("sp", I("dma_start", out=DFTC, in_=dft_c_d), writes=["DFTC"])
        S.dma("sp", I("dma_start", out=HMASK, in_=hmask_d), writes=["HMASK"])
        S.dma("sp", I("dma_start", out=RMASK, in_=rmask_d), writes=["RMASK"])
        STG = V(PH0 + 49152 + 8192, F32, [128])[0:52, :]
        ID32 = V(PH0 + 49152 + 8704, F32, [128])
        S.dma("sp", I("dma_start", out=ID32, in_=ident32_d), writes=["ID32"])
        for r, (l, w) in enumerate(((0, 0), (0, 2), (1, 0), (1, 2))):
            S.dma("sp", I("dma_start", out=STG[8 * r:8 * r + 8, :], in_=gains_d[l, w, :].rearrange("(c p) -> c p", p=128)),
                  writes=[("STG", r)])
        S.dma("sp", I("dma_start", out=STG[32:48, :], in_=lb_d.rearrange("e d (h k) -> (e d h) k", k=128)), writes=[("STG", 4)])
        S.dma("sp", I("dma_start", out=STG[48:52, :], in_=og_d), writes=[("STG", 5)])
        bst = nbank()
        S.op("pe", I("transpose", bank(bst)[:, 0:52], STG, ID32[0:52, 0:52]),
             reads=[("STG", r) for r in range(6)] + ["ID32"], writes=[("ps", bst)])
        S.op("act", I("copy", CST, bank(bst)[:, 0:52]), reads=[("ps", bst)], writes=["GPRE", "LBT", "OGAIN"])
        sink_bc = bass.AP(tensor=sink_d.tensor, offset=sink_d.offset, ap=[[0, 128], [1, 16]])
        S.dma("sp", I("dma_start", out=ESINK, in_=sink_bc), writes=["ESINK"])
        S.op("pool", I("memset", ONES, 1.0), writes=["ONES"])
        S.op("pool", I("memset", EPSC, EPS), writes=["EPSC"])
        S.op("pool", I("memset", ONEC, 1.0), writes=["ONEC"])
        S.op("dve", I("tensor_tensor", LB, LBT[:, 1, :, :], LBT[:, 0, :, :], ALU.subtract),
             reads=["LBT"], writes=["LB"])
        S.op("act", I("activation", LB, LB, AF.Exp), reads=["LB"], writes=["LB"])
        S.op("dve", I("tensor_scalar", LB, LB, 1.0, None, ALU.add), reads=["LB"], writes=["LB"])
        S.op("dve", I("reciprocal", LB, LB), reads=["LB"], writes=["LB"])
        S.op("dve", I("tensor_scalar", OML, LB, -1.0, 1.0, ALU.mult, ALU.add), reads=["LB"], writes=["OML"])
        S.op("act", I("activation", ESINK, ESINK, AF.Exp), reads=["ESINK"], writes=["ESINK"])
        S.op("act", I("activation", LNOML, OML, AF.Ln), reads=["OML"], writes=["LNOML"])

        def prenorm(tiles, grow, col_of, XN, JUNK, HNd=None, junk_keys=("JUNK", "TMPN")):
            for n_, i in enumerate(tiles):
                prenorm_tile(n_, i, grow, col_of(i), XN, JUNK, HNd, junk_keys)

        def prenorm_tile(n_, i, grow, c0, XN, JUNK, HNd=None, junk_keys=("JUNK", "TMPN"), fixed_bank=None):
            HNd = HN if HNd is None else HNd
            if True:
                S.op("act", I("activation", JUNK, X[:, i, :], AF.Square, accum_out=SS[:, i:i + 1]),
                     reads=[("X", i)], writes=list(junk_keys) + [("SS", i)])
                S.op("act", I("activation", LNV[:, i:i + 1], SS[:, i:i + 1], AF.Ln, scale=1.0 / D, bias=EPSC),
                     reads=[("SS", i), "EPSC"], writes=[("LNV", i)])
                S.op("act", I("activation", RSTD[:, i:i + 1], LNV[:, i:i + 1], AF.Exp, scale=-0.5),
                     reads=[("LNV", i)], writes=[("RSTD", i)])
                xn = XN[n_ % 2]
                S.op("dve", I("tensor_scalar", xn, X[:, i, :], RSTD[:, i:i + 1], None, ALU.mult),
                     reads=[("X", i), ("RSTD", i)], writes=[("XN", n_ % 2)])
                b = nbank() if fixed_bank is None else fixed_bank
                pb = bankb(b)
                for c in range(8):
                    S.op("pe", I("transpose", pb[:, c * 128:(c + 1) * 128], xn[:, c * 128:(c + 1) * 128], IDENT),
                         reads=[("XN", n_ % 2), "IDENT"], writes=[("ps", b)])
                g_bc = mkap(GPRE[:, grow, 0:1], [[1, 8], [0, 128]])
                S.op("dve", I("tensor_tensor",
                    HNd[:, :, c0:c0 + 128], pb.rearrange("p (c n) -> p c n", c=8), g_bc, ALU.mult),
                    reads=[("ps", b), "GPRE"], writes=[("HN", c0 // 128)])

        def postnorm(i, pair, GB, TMP, JUNK32):
            ps = pp[pair][:, :]
            S.op("act", I("activation", JUNK32, ps, AF.Square, accum_out=SS2[:, 0:1]),
                 reads=[("ps", 2 * pair), ("ps", 2 * pair + 1)], writes=["JUNK", "TMPN", "SS2"])
            S.op("act", I("activation", SS2[:, 1:2], SS2[:, 0:1], AF.Ln, scale=1.0 / D, bias=EPSC),
                 reads=["SS2", "EPSC"], writes=["SS2b"])
            S.op("act", I("activation", SS2[:, 2:3], SS2[:, 1:2], AF.Exp, scale=-0.5),
                 reads=["SS2b"], writes=["SS2c"])
            S.op("dve", I("scalar_tensor_tensor", TMP, ps, SS2[:, 2:3], GB, ALU.mult, ALU.mult),
                 reads=[("ps", 2 * pair), ("ps", 2 * pair + 1), "SS2c", "GB"], writes=["TMPN"])
            S.op("dve", I("tensor_tensor", X[:, i, :], X[:, i, :], TMP, ALU.add),
                 reads=["TMPN", ("X", i)], writes=[("X", i)])

        def load_gain_bc(GB, l, w):
            src = gains_d[l, w, :]
            src_bc = bass.AP(tensor=src.tensor, offset=src.offset, ap=[[0, 128], [1, D]])
            S.dma("sp", I("dma_start", out=GB, in_=src_bc), writes=["GB"])

        def outproj(CATv, w_d, l, GB, TMP, JUNK32, sb=8):
            for k in range(8):
                load_rowblock(w_d, k * 128, sb + k)
            load_gain_bc(GB, l, 1)
            for i in range(NT):
                pair = pair_ctr[0] % 4
                pair_ctr[0] += 1
                for half in range(2):
                    for k in range(8):
                        S.op("pe", I("matmul",
                            pp[pair][:, half * 512:(half + 1) * 512], lhsT=CATv(k)[:, i * 128:(i + 1) * 128],
                            rhs=WR[:, sb + k, half * 512:(half + 1) * 512], start=(k == 0), stop=(k == 7)),
                            reads=[("CAT", k, i // 4), ("wr", sb + k)], writes=[("ps", 2 * pair + half)])
                postnorm(i, pair, GB, TMP, JUNK32)


        def ffn(l):
            HN2 = V(HN0, BF16, [8, 1024])
            GTb = V(90112, BF16, [NFC, 1024])
            W2b = V(135168, BF16, [NFC, 1024])
            WRF = V(180224, BF16, [8, 1024])
            XNf = [V(196608, BF16, [D]), V(198656, BF16, [D])]
            SG = V(200704, F32, [512])
            TMPf = V(202752, F32, [D])
            JUNKf = V(202752, BF16, [D])
            GBf = V(206848, F32, [D])
            load_gain_bc(GBf, l, 3)
            JUNKs = V(200704, BF16, [D])
            cnt = 0
            prenorm(range(8), 1 + 2 * l, lambda i: i * 128, XNf, JUNKs, HNd=HN2, junk_keys=("SG",))
            for half in range(2):
                tiles = range(8 * half, 8 * half + 8)
                for fc in range(NFC):
                    slots = []
                    for wi, wd in enumerate((w1_d, w3_d)):
                        slot = cnt % 8
                        cnt += 1
                        slots.append(slot)
                        src = wd[l, :, fc * 128:(fc + 1) * 128].rearrange("(c p) n -> p c n", p=128)
                        dst = WRF[:, slot, :].rearrange("p (c n) -> p c n", c=8)
                        S.dma("pool", I("dma_start", out=dst, in_=src), writes=[("wrf", slot)])
                    if half == 0:
                        S.dma("pool", I("dma_start", out=W2b[:, fc, :], in_=w2_d[l, fc * 128:(fc + 1) * 128, :]), writes=[("w2", fc)])
                    for tb in range(2):
                        ba, bb = nbank(), nbank()
                        for wi, bnk in ((0, ba), (1, bb)):
                            wv = WRF[:, slots[wi], :].rearrange("p (c n) -> p c n", c=8)
                            for k in range(8):
                                S.op("pe", I("matmul", bank(bnk), lhsT=wv[:, k, :], rhs=HN2[:, k, tb * 512:(tb + 1) * 512],
                                             start=(k == 0), stop=(k == 7)),
                                     reads=[("wrf", slots[wi])] + [("HN", 4 * tb + j) for j in range(4)], writes=[("ps", bnk)])
                        S.op("act", I("activation", SG, bank(ba), AF.Silu), reads=[("ps", ba)], writes=["SG"])
                        S.op("dve", I("tensor_tensor", GTb[:, fc, tb * 512:(tb + 1) * 512], SG, bank(bb), ALU.mult),
                             reads=["SG", ("ps", bb)], writes=[("gt", fc, tb)])
                for i in tiles:
                    tl = i - 8 * half
                    pair = pair_ctr[0] % 3
                    pair_ctr[0] += 1
                    for hc in range(2):
                        for fc in range(NFC):
                            S.op("pe", I("matmul", pp[pair][:, hc * 512:(hc + 1) * 512], lhsT=GTb[:, fc, tl * 128:(tl + 1) * 128],
                                         rhs=W2b[:, fc, hc * 512:(hc + 1) * 512], start=(fc == 0), stop=(fc == NFC - 1)),
                                 reads=[("gt", fc, tl // 4), ("w2", fc)], writes=[("ps", 2 * pair + hc)])
                    if half == 0:
                        prenorm_tile(tl, 8 + tl, 1 + 2 * l, tl * 128, XNf, JUNKs, HN2, ("SG",), fixed_bank=6 + (tl % 2))
                    postnorm(i, pair, GBf, TMPf, TMPf)
                    if l == 1:
                        S.dma("sp", I("dma_start", out=y_d[i * 128:(i + 1) * 128, :], in_=X[:, i, :]), reads=[("X", i)])
            S.barrier()


        def attn_layer():
            OT = V(122880, BF16, [8, T])
            BIAS = V(155648, F32, [3, 8, 128])
            LOGS = [V(203840, F32, [512]), V(205888, F32, [512])]
            lcnt = [0]
            QT = V(167936, BF16, [4, T])
            KT = V(184320, BF16, [T])
            KT1 = V(199744, BF16, [T])
            VA = V(188416, BF16, [NT, 2, 65])
            PT = [V(192576, BF16, [3, 512]), V(195648, BF16, [3, 512]), V(207936, BF16, [3, 512])]
            OTK = [V(198720, BF16, [256]), V(199232, BF16, [256]), V(211008, BF16, [256])]
            scnt = [0]
            XNa = [V(199744, BF16, [D]), V(201792, BF16, [D])]
            JUNKa = V(203840, BF16, [D])
            GBa = V(155648, F32, [D])
            TMPa = V(167936, F32, [D])
            SCALE = 64.0 ** -0.5
            prenorm(range(NT), 2, lambda i: i * 128, XNa, JUNKa)
            S.op("dve", I("memset", VA[:, :, :, 64:65], 1.0), writes=["VA1"])
            S.barrier()
            S.op("dve", I("memset", KT[64:128, :], 0.0), writes=["KTZ"])
            S.op("dve", I("memset", KT1[0:64, :], 0.0), writes=["KTZ"])
            for pi in range(2):
                for g in range(4):
                    for hl in range(2):
                        hh = 8 * pi + 4 * hl + g
                        src = w_qkv_d[:, 64 * hh:64 * hh + 64].rearrange("(c p) n -> p c n", p=128)
                        dst = WR[:, g, :].rearrange("p (c n) -> p c n", c=8)[:, :, 64 * hl:64 * hl + 64]
                        S.dma("pool", I("dma_start", out=dst, in_=src), writes=[("wr", g)])
                load_colblock(w_qkv_d, 1024 + 128 * pi, 4)
                load_colblock(w_qkv_d, 1280 + 128 * pi, 5)
                S.dma("sp", I("dma_start", out=BIAS, in_=abias_d[:, :, 8 * pi:8 * pi + 8, :]), writes=["BIAS"])
                for g in range(5):
                    for tb in range(4):
                        b = nbank()
                        for k in range(8):
                            S.op("pe", I("matmul", bank(b), lhsT=wr_cols(g)[:, k, :], rhs=HN[:, k, tb * 512:(tb + 1) * 512],
                                         start=(k == 0), stop=(k == 7)),
                                 reads=[("wr", g)], writes=[("ps", b)])
                        if g < 4:
                            S.op("act", I("copy", QT[:, g, tb * 512:(tb + 1) * 512], bank(b)), reads=[("ps", b)],
                                 writes=[("QT", g, tb)])
                        else:
                            S.op("act", I("copy", KT[0:64, tb * 512:(tb + 1) * 512], bank(b)[0:64, :]), reads=[("ps", b), "KTZ"],
                                 writes=[("KT", tb)])
                            S.op("act", I("copy", KT1[64:128, tb * 512:(tb + 1) * 512], bank(b)[64:128, :]), reads=[("ps", b), "KTZ"],
                                 writes=[("KT1", tb)])
                for t4 in range(4):
                    b = nbank()
                    for tl in range(4):
                        ti = t4 * 4 + tl
                        for k in range(8):
                            S.op("pe", I("matmul", bank(b)[:, tl * 128:(tl + 1) * 128], lhsT=HN[:, k, ti * 128:(ti + 1) * 128],
                                         rhs=wr_cols(5)[:, k, :], start=(k == 0), stop=(k == 7)),
                                 reads=[("wr", 5)], writes=[("ps", b)])
                    S.op("act", I("copy", VA[:, t4 * 4:t4 * 4 + 4, :, 0:64],
                                  bank(b).rearrange("p (a b c) -> p a b c", a=4, b=2)),
                         reads=[("ps", b), "VA1"], writes=[("VA", t4)])
                iters = [(j, kl) for j in range(NT) for kl in range(2)]

                def stA(i):
                    j, kl = iters[i]
                    pb = 64 * kl
                    par = i % 3
                    for o in (-1, 0, 1):
                        if not (0 <= j + o < NT):
                            continue
                        oi = o + 1
                        jj = j + o
                        b = scnt[0] % 4
                        scnt[0] += 1
                        rhs = mkap(QT[:, 0, j * 128:j * 128 + 1], [[T, 4], [1, 128]])
                        ktz = KT if kl == 0 else KT1
                        S.op("pe", I("matmul", bank(b), lhsT=ktz[:, jj * 128:(jj + 1) * 128], rhs=rhs,
                                     start=True, stop=True),
                             reads=[("KT", jj // 4), ("KT1", jj // 4), "KTZ"] + [("QT", g, j // 4) for g in range(4)],
                             writes=[("ps", b)])
                        lg = LOGS[lcnt[0] % 2]
                        lk = ("LOG", lcnt[0] % 2)
                        lcnt[0] += 1
                        S.op("dve", I("tensor_tensor", lg, bank(b),
                                      BIAS[:, oi, 4 * kl:4 * kl + 4, :].rearrange("p a b -> p (a b)"), ALU.add),
                             reads=[("ps", b), "BIAS"], writes=[lk])
                        S.op("act", I("activation", PT[par][:, oi, :], lg, AF.Exp, scale=SCALE),
                             reads=[lk], writes=[("PT", par, oi)])

                def stB(i):
                    j, kl = iters[i]
                    kk = 2 * pi + kl
                    par = i % 3
                    offs = [o for o in (-1, 0, 1) if 0 <= j + o < NT]
                    bo = 4 + (i % 2)
                    for g in range(4):
                        for n_, o in enumerate(offs):
                            S.op("pe", I("matmul", bank(bo)[:, g * 65:(g + 1) * 65],
                                         lhsT=PT[par][:, o + 1, g * 128:(g + 1) * 128], rhs=VA[:, j + o, kl, :],
                                         start=(n_ == 0), stop=(n_ == len(offs) - 1)),
                                 reads=[("VA", (j + o) // 4), ("PT", par, o + 1)], writes=[("ps", bo)])
                    den = DEN3[:, 4 * par:4 * par + 4]
                    dps = mkap(bank(bo)[:, 64:65], [[65, 4]])
                    S.op("dve", I("tensor_tensor", den, dps, ESINK[:, 4 * kk:4 * kk + 4], ALU.add),
                         reads=[("ps", bo), "ESINK"], writes=[("DEN", par)])
                    S.op("dve", I("reciprocal", den, den), reads=[("DEN", par)], writes=[("DEN", par)])
                    ov = mkap(bank(bo)[:, 0:1], [[65, 4], [1, 64]])
                    dbc = mkap(den[:, 0:1], [[1, 4], [0, 64]])
                    S.op("dve", I("tensor_tensor", OTK[par].rearrange("p (a b) -> p a b", a=4), ov, dbc, ALU.mult),
                         reads=[("ps", bo), ("DEN", par)], writes=[("OTK", par)])

                def stC(i):
                    j, kl = iters[i]
                    kk = 2 * pi + kl
                    par = i % 3
                    bt = 6 + (i % 2)
                    ptb = bankb(bt)
                    for hp in range(2):
                        S.op("pe", I("transpose", ptb[:, hp * 128:(hp + 1) * 128], OTK[par][:, hp * 128:(hp + 1) * 128], IDENT),
                             reads=[("OTK", par), "IDENT"], writes=[("ps", bt)])
                    S.op("act", I("copy", OT[:, 2 * kk:2 * kk + 2, j * 128:(j + 1) * 128],
                                  ptb[:, 0:256].rearrange("p (a b) -> p a b", a=2)),
                         reads=[("ps", bt)], writes=[("CAT", 2 * kk, j // 4), ("CAT", 2 * kk + 1, j // 4)])

                nI = len(iters)
                for i in range(nI + 2):
                    if i < nI:
                        stA(i)
                    if 0 <= i - 1 < nI:
                        stB(i - 1)
                    if 0 <= i - 2 < nI:
                        stC(i - 2)
            S.barrier()
            outproj(lambda k: OT[:, k, :], w_co_d, 1, GBa, TMPa, TMPa, sb=0)
            S.barrier()

        OF = V(PH0, F32, [4, T])
        CATF = V(PH0 + 32768, BF16, [4, T])
        TM0 = PH0 + 49152

        def catv(k):
            if k < 4:
                return V(PH0 + 8192 * k + 4096, BF16, [T])
            return CATF[:, k - 4, :]
        XN = [V(TM0, BF16, [D]), V(TM0 + 2048, BF16, [D])]
        JUNK = V(TM0 + 4096, BF16, [D])
        prenorm(range(NT), 0, lambda i: i * 128, XN, JUNK)
        if dbg == "hn0":
            dump("hn0", HN, BF16, [8, T])
            dump("rstd", RSTD, F32, [16])
            dump("gpre", GPRE, F32, [4, 8])
            return fin()
        S.barrier()

        AB = V(PH0, BF16, [4, NT, 256])
        TABR = V(TM0, BF16, [4, 2, 512])
        UT = V(TM0 + 8192, BF16, [4, 512])
        for g in range(4):
            load_colblock(w_in_d, 2560 + g * 128, 12 + g)
        for bk in range(4):
            cs = slice(bk * 512, (bk + 1) * 512)
            for g in range(4):
                bu_ = nbank()
                ups = bank(bu_)
                for k in range(8):
                    S.op("pe", I("matmul", ups, lhsT=wr_cols(12 + g)[:, k, :], rhs=HN[:, k, cs],
                                                                     start=(k == 0), stop=(k == 7)),
                         reads=[("wr", 12 + g)], writes=[("ps", bu_)])
                S.op("act", I("copy", UT[:, g, :], ups), reads=[("ps", bu_)], writes=[("UT", g)])
                ba = nbank()
                aps_ = bank(ba)
                ba2 = nbank()
                aps2 = bank(ba2)
                for tl in range(4):
                    tgt = (aps_ if tl < 2 else aps2)[:, (tl % 2) * 256:(tl % 2) * 256 + 256]
                    S.op("pe", I("matmul", tgt, lhsT=UT[:, g, tl * 128:(tl + 1) * 128], rhs=DFTC[:, 0:256], start=True, stop=True),
                         reads=[("UT", g), "DFTC"], writes=[("ps", ba if tl < 2 else ba2)])
                S.op("act", I("copy", AB[:, g, bk * 4:bk * 4 + 2, :].rearrange("p a b -> p (a b)"), aps_),
                     reads=[("ps", ba)], writes=[("AB", g, bk)])
                S.op("act", I("copy", AB[:, g, bk * 4 + 2:bk * 4 + 4, :].rearrange("p a b -> p (a b)"), aps2),
                     reads=[("ps", ba2)], writes=[("AB", g, bk)])
        tabv = dft_t_d.rearrange("(c p) s k -> p c s k", p=128)
        QS = [V(TM0 + 12288, F32, [512]), V(TM0 + 14336, F32, [512])]
        allcat = lambda g: [("CAT", 4 + g, b_) for b_ in range(4)]
        bmid = nbank()
        for g in range(4):
            for c in range(NT):
                S.op("pe", I("matmul", bank(bmid)[:, 2 * g:2 * g + 2], lhsT=AB[:, g, c, 0:128], rhs=DFTC[:, 256:258],
                             start=(c == 0), stop=(c == NT - 1)),
                     reads=[("AB", g, c // 4), "DFTC"], writes=[("ps", bmid)])
        S.op("act", I("copy", CATF[:, :, 1024:1025], mkap(bank(bmid)[:, 0:1], [[2, 4], [1, 1]])),
             reads=[("ps", bmid)], writes=[k_ for g in range(4) for k_ in allcat(g)])
        tcount = 0
        for kb in range(2):
            ks = slice(kb * 512, (kb + 1) * 512)
            for c in range(NT):
                slot = tcount % 4
                tcount += 1
                S.dma("sp", I("dma_start", out=TABR[:, slot, :, :], in_=tabv[:, c, :, ks]),
                      writes=[("tab", slot)])
                for g in range(4):
                    S.op("pe", I("matmul", bank(g), lhsT=AB[:, g, c, 0:128], rhs=TABR[:, slot, 0, :],
                                 start=(c == 0), stop=(c == NT - 1)),
                         reads=[("AB", g, c // 4), ("tab", slot)], writes=[("ps", g)])
                    S.op("pe", I("matmul", bank(4 + g), lhsT=AB[:, g, c, 128:256], rhs=TABR[:, slot, 1, :],
                                 start=(c == 0), stop=(c == NT - 1)),
                         reads=[("AB", g, c // 4), ("tab", slot)], writes=[("ps", 4 + g)])
            for g in range(4):
                qs = QS[g % 2]
                S.op("act", I("copy", qs, bank(4 + g)), reads=[("ps", 4 + g)], writes=[("QS", g % 2)])
                S.op("dve", I("tensor_tensor", CATF[:, g, ks], bank(g), qs, ALU.add),
                     reads=[("ps", g), ("QS", g % 2)], writes=allcat(g))
                if kb == 0:
                    mo = mkap(CATF[:, g, 2047:2048], [[-1, 511]])
                    S.op("dve", I("tensor_tensor", mo, bank(g)[:, 1:512], qs[:, 1:512], ALU.subtract),
                         reads=[("ps", g), ("QS", g % 2)], writes=allcat(g))
                else:
                    mo = mkap(CATF[:, g, 1536:1537], [[-1, 512]])
                    S.op("dve", I("tensor_tensor", mo, bank(g), qs, ALU.subtract),
                         reads=[("ps", g), ("QS", g % 2)], writes=allcat(g))
        if dbg == "fnet":
            dump("catf", CATF, BF16, [4, T])
            dump("ab", AB, BF16, [4, NT, 256])
            return fin()
        S.barrier()

        o_ = [TM0]

        def talloc(dt, shape):
            n = 1
            for s_ in shape:
                n *= s_
            nb = n * (4 if dt == F32 else 2)
            ap = V(o_[0], dt, shape)
            o_[0] += nb
            assert o_[0] <= ARENA_B, o_[0]
            return ap

        T1 = talloc(F32, [512])
        T2 = talloc(F32, [512])
        T3 = talloc(F32, [512])
        T4 = talloc(F32, [512])
        T6 = talloc(F32, [512])
        T7 = talloc(F32, [512])
        QA = talloc(BF16, [512])
        Q2 = talloc(BF16, [512])
        Q3 = talloc(BF16, [512])
        QTS = talloc(BF16, [512])
        KA = talloc(BF16, [512])
        KOUT = talloc(BF16, [512])
        KTET = talloc(BF16, [512])
        KTE = talloc(BF16, [4, 128])
        VT = talloc(BF16, [4, 128])
        PM = [talloc(BF16, [128]) for _ in range(4)]
        SBFL = [talloc(BF16, [128]) for _ in range(3)]

        QSC = 128.0 ** -0.5
        for h in range(4):
            load_colblock(w_in_d, 0 + h * 128, 0 + h)
            load_colblock(w_in_d, 512 + h * 128, 4 + h)
            load_colblock(w_in_d, 1536 + h * 128, 8 + h)

        def rev512(ap2d):
            base = ap2d[:, 511:512]
            return mkap(base, [[-1, 512]])

        def hgrn_sweep(dr):
            fwd = (dr == 0)
            blocks = range(4) if fwd else range(3, -1, -1)
            for h in range(4):
                S.op("dve", I("memset", S32[:, h, :], 0.0), reads=[("SBF", h)], writes=[("S32", h)])
            first_tile = 0 if fwd else NT - 1
            its = [(bk_, h_) for bk_ in blocks for h_ in range(4)]

            def emit_z(n_it):
                bk_, h_ = its[n_it]
                zb = 0 if n_it % 2 == 0 else 7
                cs_ = slice(bk_ * 512, (bk_ + 1) * 512)
                for k in range(8):
                    S.op("pe", I("matmul", bank(zb), lhsT=wr_cols(4 + h_)[:, k, :], rhs=HN[:, k, cs_],
                                 start=(k == 0), stop=(k == 7)),
                         reads=[("wr", 4 + h_)], writes=[("ps", zb)])
                return zb

            zbanks = {0: emit_z(0)}
            for n_cur, (bk, h) in enumerate(its):
                cs = slice(bk * 512, (bk + 1) * 512)
                if True:
                    bz, bq, bv = zbanks[n_cur], 1, 2
                    zps, qps, vps = bank(bz), bank(bq), bank(bv)
                    for k in range(8):
                        S.op("pe", I("matmul", qps, lhsT=wr_cols(h)[:, k, :], rhs=HN[:, k, cs],
                                                                         start=(k == 0), stop=(k == 7)),
                             reads=[("wr", h)], writes=[("ps", bq)])
                    for tl in range(4):
                        ti = bk * 4 + tl
                        for k in range(8):
                            S.op("pe", I("matmul",
                                vps[:, tl * 128:(tl + 1) * 128], lhsT=HN[:, k, ti * 128:(ti + 1) * 128],
                                rhs=wr_cols(8 + h)[:, k, :], start=(k == 0), stop=(k == 7)),
                                reads=[("wr", 8 + h)], writes=[("ps", bv)])
                    bt = 3
                    ptb = bankb(bt)

                    def chain(hb):
                        sl = slice(256 * hb, 256 * hb + 256)
                        K = lambda n: (n, hb)
                        t1, t2, t3, t4, t6, t7 = (x[:, sl] for x in (T1, T2, T3, T4, T6, T7))
                        qa, q2, q3, qts, ka, kout, ktet = (x[:, sl] for x in (QA, Q2, Q3, QTS, KA, KOUT, KTET))
                        zp, qp = zps[:, sl], qps[:, sl]
                        S.op("act", I("activation", t1, zp, AF.Exp, scale=-1.0), reads=[("ps", bz)], writes=[K("T1")])
                        yield
                        S.op("act", I("activation", t7, t1, AF.Ln, bias=ONEC), reads=[K("T1"), "ONEC"], writes=[K("T7")])
                        yield
                        S.op("act", I("activation", t2, t1, AF.Ln, scale=LB[:, dr, h:h + 1], bias=ONEC),
                             reads=[K("T1"), "ONEC", "LB"], writes=[K("T2")])
                        yield
                        S.op("dve", I("tensor_tensor", t2, t2, t7, ALU.subtract), reads=[K("T2"), K("T7")], writes=[K("T2")])
                        yield
                        S.op("dve", I("scalar_tensor_tensor", t1, zp, -1.0, t7, ALU.mult, ALU.subtract),
                             reads=[("ps", bz), K("T7"), K("T1")], writes=[K("T1")])
                        yield
                        if fwd:
                            g_o, g_i, gt_o = t3, t2, t4
                        else:
                            rv = lambda x: mkap(x[:, 255:256], [[-1, 256]])
                            g_o, g_i, gt_o = rv(t3), rv(t2), rv(t4)
                        S.op("dve", I("tensor_tensor_scan", g_o, RMASK[:, 0, 0:256], g_i, 0.0, ALU.mult, ALU.add),
                             reads=[K("T2"), "RMASK"], writes=[K("T3")])
                        yield
                        S.op("dve", I("tensor_tensor_scan", gt_o, RMASK[:, 1, 0:256], g_i, 0.0, ALU.mult, ALU.add),
                             reads=[K("T2"), "RMASK"], writes=[K("T4")])
                        yield
                        S.op("act", I("activation", t2, t3, AF.Exp), reads=[K("T3")], writes=[K("T2")])
                        yield
                        S.op("act", I("activation", t6, t4, AF.Exp), reads=[K("T4")], writes=[K("T6")])
                        yield
                        S.op("dve", I("tensor_tensor", t3, t1, t3, ALU.subtract), reads=[K("T1"), K("T3"), K("T2")], writes=[K("T3")])
                        yield
                        S.op("act", I("activation", ka, t3, AF.Exp, bias=LNOML[:, dr, h:h + 1]),
                             reads=[K("T3"), "LNOML"], writes=[K("KA")])
                        yield
                        lt_pos = 127 if fwd else 0
                        lt_bc = mkap(t4[:, lt_pos:lt_pos + 1], [[128, 2], [0, 128]])
                        S.op("dve", I("tensor_tensor", t7, t1, t4, ALU.subtract), reads=[K("T1"), K("T4"), K("T7")], writes=[K("T7")])
                        yield
                        S.op("dve", I("tensor_tensor",
                            t7.rearrange("p (a b) -> p a b", a=2), t7.rearrange("p (a b) -> p a b", a=2), lt_bc, ALU.add),
                            reads=[K("T4"), K("T7")], writes=[K("T7")])
                        yield
                        S.op("act", I("activation", ktet, t7, AF.Exp, bias=LNOML[:, dr, h:h + 1]),
                             reads=[K("T7"), "LNOML"], writes=[K("KTET")])
                        yield
                        S.op("dve", I("scalar_tensor_tensor", qa, qp, QSC, t2, ALU.mult, ALU.mult),
                             reads=[("ps", bq), K("T2")], writes=[K("QA")])
                        yield
                        S.op("dve", I("scalar_tensor_tensor", qts, qp, QSC, t6, ALU.mult, ALU.mult),
                             reads=[("ps", bq), K("T6")], writes=[K("QTS")])
                        yield
                        dpos = 31 if fwd else 0
                        if fwd:
                            d1 = mkap(t2[:, dpos:dpos + 1], [[32, 7], [0, 32]])
                            d2 = mkap(t2[:, dpos:dpos + 1], [[32, 6], [0, 32]])
                            q2o, q2i = q2[:, 32:256], qa[:, 32:256]
                            q3o, q3i = q3[:, 64:256], q2[:, 64:256]
                        else:
                            d1 = mkap(t2[:, 32:33], [[32, 7], [0, 32]])
                            d2 = mkap(t2[:, 64:65], [[32, 6], [0, 32]])
                            q2o, q2i = q2[:, 0:224], qa[:, 0:224]
                            q3o, q3i = q3[:, 0:192], q2[:, 0:192]
                        S.op("dve", I("tensor_tensor",
                            q2o.rearrange("p (a b) -> p a b", b=32), q2i.rearrange("p (a b) -> p a b", b=32), d1, ALU.mult),
                            reads=[K("QA"), K("T2")], writes=[K("Q2")])
                        yield
                        S.op("dve", I("tensor_tensor",
                            q3o.rearrange("p (a b) -> p a b", b=32), q3i.rearrange("p (a b) -> p a b", b=32), d2, ALU.mult),
                            reads=[K("Q2"), K("T2")], writes=[K("Q3")])
                        yield
                        down = mkap(t2[:, dpos:dpos + 1], [[32, 8], [0, 32]])
                        S.op("dve", I("tensor_tensor",
                            kout.rearrange("p (a b) -> p a b", b=32), ka.rearrange("p (a b) -> p a b", b=32), down, ALU.mult),
                            reads=[K("KA"), K("T2")], writes=[K("KOUT")])
                        yield
                        for t2l in range(2):
                            tl_ = 2 * hb + t2l
                            S.op("pe", I("transpose", ptb[:, tl_ * 128:(tl_ + 1) * 128], KTET[:, tl_ * 128:(tl_ + 1) * 128], IDENT),
                                 reads=[K("KTET"), "IDENT"], writes=[("ps", bt)])
                        S.op("act", I("copy", KTE[:, 2 * hb:2 * hb + 2, :].rearrange("p a b -> p (a b)"), ptb[:, sl]),
                             reads=[("ps", bt)], writes=[K("KTE")])
                        yield

                    gens = [chain(hb_) for hb_ in ((0, 1) if fwd else (1, 0))]
                    while gens:
                        for g_ in list(gens):
                            try:
                                next(g_)
                            except StopIteration:
                                gens.remove(g_)
                    S.op("act", I("copy", VT.rearrange("p a b -> p (a b)"), vps),
                         reads=[("ps", bv)], writes=["VT"])
                    if n_cur + 1 < len(its):
                        zbanks[n_cur + 1] = emit_z(n_cur + 1)
                    bs, bo, bu = 4, 5, 6
                    sps, ops_, ups = bank(bs), bank(bo), bank(bu)
                    tls = list(range(4)) if fwd else list(range(3, -1, -1))
                    mk = HMASK[:, 0:128] if fwd else HMASK[:, 128:256]
                    for tl in tls:
                        c0 = tl * 128
                        hb_t = tl // 2
                        sc_ = sps[:, c0:c0 + 128]
                        S.op("pe", I("matmul", sc_, lhsT=KA[:, c0:c0 + 128], rhs=QA[:, c0:c0 + 128], start=True, stop=True),
                             reads=[("KA", hb_t), ("QA", hb_t)], writes=[("ps", bs)])
                        for dd, Qd in ((1, QA), (2, Q2), (3, Q3)):
                            for a in range(4):
                                b_ = a - dd if fwd else a + dd
                                if b_ < 0 or b_ > 3:
                                    continue
                                kw = {}
                                if b_ == 3:
                                    kw["tile_position"] = (0, 96)
                                S.op("pe", I("matmul",
                                    sps[32 * b_:32 * b_ + 32, c0 + 32 * a:c0 + 32 * a + 32],
                                    lhsT=KOUT[:, c0 + 32 * b_:c0 + 32 * b_ + 32], rhs=Qd[:, c0 + 32 * a:c0 + 32 * a + 32],
                                    start=True, stop=True, **kw),
                                    reads=[("KOUT", hb_t), ("QA", hb_t), ("Q2", hb_t), ("Q3", hb_t)], writes=[("ps", bs)])
                    for tl in tls:
                        c0 = tl * 128
                        S.op("dve", I("tensor_tensor", PM[tl], sps[:, c0:c0 + 128], mk, ALU.mult),
                             reads=[("ps", bs), "HMASK"], writes=[("PM", tl)])
                    for tl in tls:
                        c0 = tl * 128
                        S.op("pe", I("matmul", ups[:, c0:c0 + 128], lhsT=KTE[:, tl, :], rhs=VT[:, tl, :], start=True, stop=True,
                                     skip_group_check=True),
                             reads=[("KTE", tl // 2), "VT"], writes=[("ps", bu)])
                    for n_, tl in enumerate(tls):
                        c0 = tl * 128
                        S.op("pe", I("matmul", ops_[:, c0:c0 + 128], lhsT=VT[:, tl, :], rhs=PM[tl], start=(n_ == 0), stop=True,
                                     skip_group_check=True),
                             reads=["VT", ("PM", tl)], writes=[("ps", bo)])
                    for n_, tl in enumerate(tls):
                        c0 = tl * 128
                        ti = bk * 4 + tl
                        if ti != first_tile:
                            sprev = SBF[:, h, :] if n_ == 0 else SBFL[n_ - 1]
                            skey = ("SBF", h) if n_ == 0 else ("SBFL", n_ - 1)
                            S.op("pe", I("matmul", ops_[:, c0:c0 + 128], lhsT=sprev, rhs=QTS[:, c0:c0 + 128], start=False, stop=True,
                                         skip_group_check=True),
                                 reads=[skey, ("QTS", tl // 2)], writes=[("ps", bo)])
                        dtp = (c0 + 127) if fwd else c0
                        S.op("dve", I("scalar_tensor_tensor",
                            S32[:, h, :], S32[:, h, :], T6[:, dtp:dtp + 1], ups[:, c0:c0 + 128], ALU.mult, ALU.add),
                            reads=[("S32", h), ("T6", tl // 2), ("ps", bu)], writes=[("S32", h)])
                        if n_ < 3:
                            S.op("act", I("copy", SBFL[n_], S32[:, h, :]), reads=[("S32", h)], writes=[("SBFL", n_)])
                        else:
                            S.op("act", I("copy", SBF[:, h, :], S32[:, h, :]), reads=[("S32", h)], writes=[("SBF", h)])
                    if fwd:
                        S.op("act", I("copy", OF[:, h, cs], ops_),
                             reads=[("ps", bo)], writes=[("OF", h, bk)])
                    else:
                        S.op("dve", I("tensor_tensor", T1, ops_, OF[:, h, cs], ALU.add),
                             reads=[("ps", bo), ("OF", h, bk)], writes=[("T1", 0), ("T1", 1)])
                        S.op("act", I("activation", Q2, T1, AF.Square), reads=[("T1", 0), ("T1", 1)], writes=[("Q2", 0), ("Q2", 1)])
                        bn = 4
                        nps = bank(bn)
                        S.op("pe", I("matmul", nps, lhsT=ONES, rhs=Q2, start=True, stop=True),
                             reads=["ONES", ("Q2", 0), ("Q2", 1)], writes=[("ps", bn)])
                        S.op("act", I("activation", T2, nps, AF.Ln, scale=1.0 / 128, bias=EPSC),
                             reads=[("ps", bn), "EPSC"], writes=[("T2", 0), ("T2", 1)])
                        S.op("act", I("activation", T2, T2, AF.Exp, scale=-0.5), reads=[("T2", 0), ("T2", 1)], writes=[("T2", 0), ("T2", 1)])
                        S.op("dve", I("scalar_tensor_tensor", T3, T1, OGAIN[:, h:h + 1], T2, ALU.mult, ALU.mult),
                             reads=[("T1", 0), ("T1", 1), ("T2", 0), ("T2", 1), "OGAIN"], writes=[("T3", 0), ("T3", 1)])
                        bg = 3
                        gps = bank(bg)
                        for k in range(8):
                            S.op("pe", I("matmul", gps, lhsT=wr_cols(12 + h)[:, k, :], rhs=HN[:, k, cs],
                                                                             start=(k == 0), stop=(k == 7)),
                                 reads=[("wr", 12 + h)], writes=[("ps", bg)])
                        S.op("act", I("activation", T4, gps, AF.Exp, scale=-1.0),
                             reads=[("ps", bg)], writes=[("T4", 0), ("T4", 1)])
                        S.op("act", I("activation", T4, T4, AF.Ln, bias=ONEC), reads=[("T4", 0), ("T4", 1), "ONEC"], writes=[("T4", 0), ("T4", 1)])
                        S.op("act", I("activation", T4, T4, AF.Exp, scale=-1.0), reads=[("T4", 0), ("T4", 1)], writes=[("T4", 0), ("T4", 1)])
                        S.op("dve", I("tensor_tensor", T4, gps, T4, ALU.mult), reads=[("T4", 0), ("T4", 1), ("ps", bg)], writes=[("T4", 0), ("T4", 1)])
                        S.op("dve", I("tensor_tensor", catv(h)[:, cs], T3, T4, ALU.mult),
                             reads=[("T3", 0), ("T3", 1), ("T4", 0), ("T4", 1)], writes=[("CAT", h, bk)])

        hgrn_sweep(0)
        for h in range(4):
            load_colblock(w_in_d, 1024 + h * 128, 4 + h)
            load_colblock(w_in_d, 2048 + h * 128, 12 + h)
        hgrn_sweep(1)
        S.barrier()

        GB = V(TM0 + 12288, F32, [D])
        TMPN = V(TM0 + 16384, F32, [D])
        JUNK32 = V(TM0 + 20480, BF16, [D])
        outproj(catv, w_abo_d, 0, GB, TMPN, JUNK32)
        S.barrier()

        if stage >= 2:
            ffn(0)
        if stage >= 3:
            attn_layer()
        if stage >= 4:
            ffn(1)
        S.barrier()
        if stage < 4:
            for i in range(NT):
                S.dma("sp", I("dma_start", out=y_d[i * 128:(i + 1) * 128, :], in_=X[:, i, :]),
                      reads=[("X", i)])
        S.emit(nc)
    nc._sched_stats = S.stats
    return nc


_NAMES = ["x", "norm_gains", "ab_w_in", "ab_lb_table", "ab_out_gain", "ab_w_out", "c_w_qkv", "c_sink",
          "c_w_out", "ffn_w1", "ffn_w3", "ffn_w2"]


def make_in_maps(inputs):
    c = _consts()
    f = lambda a: np.ascontiguousarray(np.asarray(a, dtype=np.float32))
    shared = {
        "norm_gains": f(inputs["norm_gains"]),
        "ab_w_in": f(inputs["ab_w_in"])[0],
        "ab_lb_table": f(inputs["ab_lb_table"]),
        "ab_out_gain": f(inputs["ab_out_gain"])[0],
        "ab_w_out": f(inputs["ab_w_out"])[0],
        "c_w_qkv": f(inputs["c_w_qkv"])[0],
        "c_sink": f(inputs["c_sink"]),
        "c_w_out": f(inputs["c_w_out"])[0],
        "ffn_w1": f(inputs["ffn_w1"]),
        "ffn_w3": f(inputs["ffn_w3"]),
        "ffn_w2": f(inputs["ffn_w2"]),
        "k_ident": c["ident"], "k_ident32": c["ident32"], "k_dft_t": c["dft_t"], "k_dft_c": c["dft_c"],
        "k_hmask": c["hmask"], "k_rmask": c["rmask"], "k_abias": c["abias"],
    }
    x = f(inputs["x"])
    return [dict(shared, x=x[b]) for b in range(x.shape[0])]


def kernel(**inputs):
    nc = build()
    in_maps = make_in_maps(inputs)
    res = run_bass_kernel_spmd(nc, in_maps, core_ids=list(range(8)))
    return np.stack([np.asarray(r["y"], dtype=np.float32) for r in res.results], axis=0)
```

```python
import math
from contextlib import ExitStack

import numpy as np
import ml_dtypes
import concourse.bass as bass
import concourse.mybir as mybir
from concourse.bass_utils import run_bass_kernel_spmd

F32 = mybir.dt.float32
BF16 = mybir.dt.bfloat16
AF = mybir.ActivationFunctionType
ALU = mybir.AluOpType

T = 2048
D = 1024
NT = 16
DFF = 2816
NFC = 22
EPS = 1e-6

ENGS = ("pe", "act", "dve", "pool", "sp")
NDMA_SLOTS = 32


class Op:
    __slots__ = ("eng", "fn", "deps", "is_dma", "slot", "dval", "needed", "sval", "idx")

    def __init__(self, eng, fn, is_dma):
        self.eng = eng
        self.fn = fn
        self.deps = []
        self.is_dma = is_dma
        self.slot = None
        self.dval = None
        self.needed = False
        self.sval = None


class Sched:
    def __init__(self):
        self.ops = {e: [] for e in ENGS}
        self.last_w = {}
        self.readers = {}
        self.ndma = 0
        self.slot_last = [None] * NDMA_SLOTS
        self.slot_cnt = [0] * NDMA_SLOTS
        self.bar = []
        self.pool_dmas = []
        self.nq = [0, 0]

    def barrier(self):
        b = []
        for e in ENGS:
            for op in reversed(self.ops[e]):
                if not op.is_dma:
                    b.append(op)
                    break
        for s in range(NDMA_SLOTS):
            if self.slot_last[s] is not None:
                b.append(self.slot_last[s])
        self.bar = b

    def _add(self, eng, fn, reads, writes, is_dma, nobar=False):
        op = Op(eng, fn, is_dma)
        deps = []
        for r in reads:
            w = self.last_w.get(r)
            if w is not None:
                deps.append(w)
        for w_ in writes:
            w = self.last_w.get(w_)
            if w is not None:
                deps.append(w)
            deps.extend(self.readers.get(w_, ()))
        if not nobar:
            deps.extend(self.bar)
        seen = set()
        latest = {}
        for d in deps:
            if d is op or id(d) in seen:
                continue
            seen.add(id(d))
            if d.is_dma:
                op.deps.append(d)
                continue
            if (not is_dma) and d.eng == "pe" and eng == "pe":
                continue
            cur = latest.get(d.eng)
            if cur is None or d.idx > cur.idx:
                latest[d.eng] = d
        for d in latest.values():
            op.deps.append(d)
            d.needed = True
        for r in reads:
            self.readers.setdefault(r, []).append(op)
        for w_ in writes:
            self.last_w[w_] = op
            self.readers[w_] = []
        if is_dma and eng == "pool":
            self.pool_dmas.append(op)
            if len(self.pool_dmas) > 4:
                pd = self.pool_dmas[-5]
                if id(pd) not in seen:
                    seen.add(id(pd))
                    op.deps.append(pd)
        if is_dma:
            half = NDMA_SLOTS // 2
            qi = 1 if eng == "pool" else 0
            s = qi * half + self.nq[qi] % half
            self.nq[qi] += 1
            self.ndma += 1
            prev = self.slot_last[s]
            if prev is not None and id(prev) not in seen:
                op.deps.append(prev)
            self.slot_cnt[s] += 16
            op.slot = s
            op.dval = self.slot_cnt[s]
            self.slot_last[s] = op
        op.idx = len(self.ops[eng])
        self.ops[eng].append(op)
        return op

    def op(self, eng, fn, reads=(), writes=()):
        return self._add(eng, fn, reads, writes, False)

    def dma(self, eng, fn, reads=(), writes=(), nobar=False):
        return self._add(eng, fn, reads, writes, True, nobar)

    def emit(self, nc):
        with ExitStack() as es:
            esem = {e: es.enter_context(nc.semaphore("s_" + e)) for e in ENGS if e != "sp"}
            dsem = [es.enter_context(nc.semaphore("d%d" % i)) for i in range(NDMA_SLOTS)]
            for e in ENGS:
                c = 0
                for op in self.ops[e]:
                    if op.is_dma:
                        continue
                    if op.needed:
                        c += 1
                        op.sval = c
            self.stats = {e: (len(self.ops[e]), sum(1 for o in self.ops[e] if o.needed and not o.is_dma)) for e in ENGS}
            block = es.enter_context(nc.Block())
            hw = {"pe": block.tensor, "act": block.scalar, "dve": block.vector,
                  "pool": block.gpsimd, "sp": block.sync}

            def run(ename):
                ops = self.ops[ename]

                def body(eng):
                    known = {}
                    for op in ops:
                        for d in op.deps:
                            if d.is_dma:
                                key = ("d", d.slot)
                                sem = dsem[d.slot]
                                val = d.dval
                            else:
                                key = ("e", d.eng)
                                sem = esem[d.eng]
                                val = d.sval
                            if known.get(key, 0) >= val:
                                continue
                            eng.wait_ge(sem, val)
                            known[key] = val
                        if callable(op.fn):
                            ins = op.fn(eng)
                        else:
                            ins = getattr(eng, op.fn[0])(*op.fn[1], **op.fn[2])
                        if op.is_dma:
                            ins.then_inc(dsem[op.slot], 16)
                        elif op.needed:
                            ins.then_inc(esem[ename], 1)
                    if ename == "sp":
                        for s in range(NDMA_SLOTS):
                            if self.slot_cnt[s] > 0:
                                eng.wait_ge(dsem[s], self.slot_cnt[s])

                hw[ename](body)

            for e in ENGS:
                run(e)


def I(name, *args, **kw):
    return (name, args, kw)


def mkap(base, dims):
    return bass.AP(tensor=base.tensor, offset=base.offset,
                   ap=[list(base.ap[0])] + [list(d) for d in dims])


_CONSTS = None


def _consts():
    global _CONSTS
    if _CONSTS is not None:
        return _CONSTS
    bf = ml_dtypes.bfloat16
    c = {}
    c["ident"] = np.eye(128, dtype=np.float32).astype(bf)
    c["ident32"] = np.eye(128, dtype=np.float32)
    t = np.arange(T, dtype=np.int64)
    ph = (np.outer(t, t) % T).astype(np.float64) * (2.0 * np.pi / T)
    tab = np.stack([np.cos(ph), np.sin(ph)], axis=1) / math.sqrt(T)
    c["dft_t"] = np.ascontiguousarray(tab.astype(np.float32).astype(bf))
    cc = np.arange(128, dtype=np.int64)
    phc = (np.outer(cc, cc) % 128).astype(np.float64) * (2.0 * np.pi / 128)
    ccsc = np.concatenate([np.cos(phc), -np.sin(phc)], axis=1) / math.sqrt(128.0)
    alt = np.where(np.arange(128) % 2 == 0, 1.0, -1.0)[:, None] / math.sqrt(T)
    ccsc = np.concatenate([ccsc, alt, np.zeros((128, 1))], axis=1)
    c["dft_c"] = ccsc.astype(np.float32).astype(bf)
    s = np.arange(128)[:, None]
    q = np.arange(128)[None, :]
    mf = (s <= q).astype(np.float32)
    mb = (s >= q).astype(np.float32)
    c["hmask"] = np.concatenate([mf, mb], axis=1).astype(np.float32).astype(bf)
    j = np.arange(512)
    rm = np.stack([(j % 32 != 0), (j % 128 != 0)], axis=0).astype(np.float32)
    c["rmask"] = np.ascontiguousarray(np.broadcast_to(rm[None], (128, 2, 512))).astype(np.float32).astype(bf)
    slopes = 2.0 ** (-8.0 * np.arange(1, 17, dtype=np.float64) / 16.0)
    scale = 64.0 ** -0.5
    bias = np.zeros((128, 3, 16, 128), dtype=np.float64)
    for oi, off in enumerate((-1, 0, 1)):
        dist = np.abs(q - (s + 128 * off)).astype(np.float64)
        valid = dist <= 128
        for h in range(16):
            b = -slopes[h] * dist / scale
            bias[:, oi, h, :] = np.where(valid, b, -1.0e9)
    c["abias"] = bias.astype(np.float32)
    _CONSTS = c
    return c


def build(stage=99, dbg=False):
    nc = bass.Bass("TRN2", target_bir_lowering=False)

    def din(name, shape, dt=F32):
        return nc.dram_tensor(name, list(shape), dt, kind="ExternalInput").ap()

    x_d = din("x", [T, D])
    gains_d = din("norm_gains", [2, 4, D])
    w_in_d = din("ab_w_in", [D, 3072])
    lb_d = din("ab_lb_table", [2, 2, 512])
    og_d = din("ab_out_gain", [4, 128])
    w_abo_d = din("ab_w_out", [D, D])
    w_qkv_d = din("c_w_qkv", [D, 1536])
    sink_d = din("c_sink", [1, 16])
    w_co_d = din("c_w_out", [D, D])
    w1_d = din("ffn_w1", [2, D, DFF])
    w3_d = din("ffn_w3", [2, D, DFF])
    w2_d = din("ffn_w2", [2, DFF, D])
    ident_d = din("k_ident", [128, 128], BF16)
    ident32_d = din("k_ident32", [128, 128])
    dft_t_d = din("k_dft_t", [T, 2, T], BF16)
    dft_c_d = din("k_dft_c", [128, 258], BF16)
    hmask_d = din("k_hmask", [128, 256], BF16)
    rmask_d = din("k_rmask", [128, 2, 512], BF16)
    abias_d = din("k_abias", [128, 3, 16, 128])
    y_d = nc.dram_tensor("y", [T, D], F32, kind="ExternalOutput").ap()

    S = Sched()
    with ExitStack() as es:
        ARENA_B = 212800
        arena = es.enter_context(nc.sbuf_tensor("arena", [128, ARENA_B // 4], F32))
        pp = [es.enter_context(nc.psum_tensor("pp%d" % i, [128, 1024], F32)) for i in range(4)]

        def V(off, dt, shape):
            n = 1
            for s_ in shape:
                n *= s_
            assert off % 4 == 0
            if dt == F32:
                ap = arena[:, off // 4: off // 4 + n]
            else:
                assert n % 2 == 0
                ap = arena[:, off // 4: off // 4 + n // 2].bitcast(BF16)
            if len(shape) == 2:
                ap = ap.rearrange("p (a b) -> p a b", a=shape[0])
            elif len(shape) == 3:
                ap = ap.rearrange("p (a b c) -> p a b c", a=shape[0], b=shape[1])
            return ap

        def bank(i):
            return pp[i // 2][:, (i % 2) * 512:(i % 2) * 512 + 512]

        def bankb(i):
            return pp[i // 2][:, (i % 2) * 512:(i % 2) * 512 + 512].bitcast(BF16)

        bank_ctr = [0]
        pair_ctr = [0]

        def nbank():
            b = bank_ctr[0] % 8
            bank_ctr[0] += 1
            return b

        X0 = 0
        C0 = 65536
        HN0 = 73728
        WR0 = 106496
        PH0 = 139264
        X = V(X0, F32, [NT, D])
        HN = V(HN0, BF16, [8, T])
        WR = V(WR0, BF16, [16, 1024])
        co = [C0]

        def calloc(dt, shape):
            n = 1
            for s_ in shape:
                n *= s_
            nb = n * (4 if dt == F32 else 2)
            nb = (nb + 3) // 4 * 4
            ap = V(co[0], dt, shape)
            co[0] += nb
            assert co[0] <= HN0
            return ap

        IDENT = calloc(BF16, [128])
        ONES = calloc(BF16, [128])
        DFTC = calloc(BF16, [258])
        HMASK = calloc(BF16, [256])
        RMASK = calloc(BF16, [2, 512])
        CST = calloc(F32, [52])
        GPRE = CST[:, 0:32].rearrange("p (a b) -> p a b", a=4)
        LBT = CST[:, 32:48].rearrange("p (a b c) -> p a b c", a=2, b=2)
        LB = calloc(F32, [2, 4])
        OML = calloc(F32, [2, 4])
        OGAIN = CST[:, 48:52]
        ESINK = calloc(F32, [16])
        EPSC = calloc(F32, [1])
        ONEC = calloc(F32, [1])
        LNOML = calloc(F32, [2, 4])
        SS = calloc(F32, [16])
        LNV = calloc(F32, [16])
        RSTD = calloc(F32, [16])
        SS2 = calloc(F32, [4])
        S32 = calloc(F32, [4, 128])
        SBF = calloc(BF16, [4, 128])
        DEN = calloc(F32, [8])
        DEN3 = calloc(F32, [12])

        def wr_cols(slot):
            return WR[:, slot, :].rearrange("p (c n) -> p c n", c=8)

        def load_colblock(w2d, col0, slot, ncols=128):
            src = w2d[:, col0:col0 + ncols].rearrange("(c p) n -> p c n", p=128)
            dst = WR[:, slot, 0:8 * ncols].rearrange("p (c n) -> p c n", c=8)
            S.dma("pool", I("dma_start", out=dst, in_=src), writes=[("wr", slot)], nobar=True)

        def load_rowblock(w2d, row0, slot):
            src = w2d[row0:row0 + 128, :]
            S.dma("pool", I("dma_start", out=WR[:, slot, :], in_=src), writes=[("wr", slot)], nobar=True)

        def dump(name, ap, dt, shape):
            S.barrier()
            d = nc.dram_tensor("dbg_" + name, [128] + list(shape), dt, kind="ExternalOutput").ap()
            S.dma("sp", I("dma_start", out=d, in_=ap))

        def fin():
            S.barrier()
            S.emit(nc)
            return nc

        for i in range(NT):
            S.dma("sp", I("dma_start", out=X[:, i, :], in_=x_d[i * 128:(i + 1) * 128, :]),
                  writes=[("X", i)])
        S.dma("sp", I("dma_start", out=IDENT, in_=ident_d), writes=["IDENT"])
        S.dma("sp", I("dma_start", out=DFTC, in_=dft_c_d), writes=["DFTC"])
        S.dma("sp", I("dma_start", out=HMASK, in_=hmask_d), writes=["HMASK"])
        S.dma("sp", I("dma_start", out=RMASK, in_=rmask_d), writes=["RMASK"])
        STG = V(PH0 + 49152 + 8192, F32, [128])[0:52, :]
        ID32 = V(PH0 + 49152 + 8704, F32, [128])
        S.dma("sp", I("dma_start", out=ID32, in_=ident32_d), writes=["ID32"])
        for r, (l, w) in enumerate(((0, 0), (0, 2), (1, 0), (1, 2))):
            S.dma("sp", I("dma_start", out=STG[8 * r:8 * r + 8, :], in_=gains_d[l, w, :].rearrange("(c p) -> c p", p=128)),
                  writes=[("STG", r)])
        S.dma("sp", I("dma_start", out=STG[32:48, :], in_=lb_d.rearrange("e d (h k) -> (e d h) k", k=128)), writes=[("STG", 4)])
        S.dma("sp", I("dma_start", out=STG[48:52, :], in_=og_d), writes=[("STG", 5)])
        bst = nbank()
        S.op("pe", I("transpose", bank(bst)[:, 0:52], STG, ID32[0:52, 0:52]),
             reads=[("STG", r) for r in range(6)] + ["ID32"], writes=[("ps", bst)])
        S.op("act", I("copy", CST, bank(bst)[:, 0:52]), reads=[("ps", bst)], writes=["GPRE", "LBT", "OGAIN"])
        sink_bc = bass.AP(tensor=sink_d.tensor, offset=sink_d.offset, ap=[[0, 128], [1, 16]])
        S.dma("sp", I("dma_start", out=ESINK, in_=sink_bc), writes=["ESINK"])
        S.op("pool", I("memset", ONES, 1.0), writes=["ONES"])
        S.op("pool", I("memset", EPSC, EPS), writes=["EPSC"])
        S.op("pool", I("memset", ONEC, 1.0), writes=["ONEC"])
        S.op("dve", I("tensor_tensor", LB, LBT[:, 1, :, :], LBT[:, 0, :, :], ALU.subtract),
             reads=["LBT"], writes=["LB"])
        S.op("act", I("activation", LB, LB, AF.Exp), reads=["LB"], writes=["LB"])
        S.op("dve", I("tensor_scalar", LB, LB, 1.0, None, ALU.add), reads=["LB"], writes=["LB"])
        S.op("dve", I("reciprocal", LB, LB), reads=["LB"], writes=["LB"])
        S.op("dve", I("tensor_scalar", OML, LB, -1.0, 1.0, ALU.mult, ALU.add), reads=["LB"], writes=["OML"])
        S.op("act", I("activation", ESINK, ESINK, AF.Exp), reads=["ESINK"], writes=["ESINK"])
        S.op("act", I("activation", LNOML, OML, AF.Ln), reads=["OML"], writes=["LNOML"])

        def prenorm(tiles, grow, col_of, XN, JUNK, HNd=None, junk_keys=("JUNK", "TMPN")):
            for n_, i in enumerate(tiles):
                prenorm_tile(n_, i, grow, col_of(i), XN, JUNK, HNd, junk_keys)

        def prenorm_tile(n_, i, grow, c0, XN, JUNK, HNd=None, junk_keys=("JUNK", "TMPN"), fixed_bank=None):
            HNd = HN if HNd is None else HNd
            if True:
                S.op("act", I("activation", JUNK, X[:, i, :], AF.Square, accum_out=SS[:, i:i + 1]),
                     reads=[("X", i)], writes=list(junk_keys) + [("SS", i)])
                S.op("act", I("activation", LNV[:, i:i + 1], SS[:, i:i + 1], AF.Ln, scale=1.0 / D, bias=EPSC),
                     reads=[("SS", i), "EPSC"], writes=[("LNV", i)])
                S.op("act", I("activation", RSTD[:, i:i + 1], LNV[:, i:i + 1], AF.Exp, scale=-0.5),
                     reads=[("LNV", i)], writes=[("RSTD", i)])
                xn = XN[n_ % 2]
                S.op("dve", I("tensor_scalar", xn, X[:, i, :], RSTD[:, i:i + 1], None, ALU.mult),
                     reads=[("X", i), ("RSTD", i)], writes=[("XN", n_ % 2)])
                b = nbank() if fixed_bank is None else fixed_bank
                pb = bankb(b)
                for c in range(8):
                    S.op("pe", I("transpose", pb[:, c * 128:(c + 1) * 128], xn[:, c * 128:(c + 1) * 128], IDENT),
                         reads=[("XN", n_ % 2), "IDENT"], writes=[("ps", b)])
                g_bc = mkap(GPRE[:, grow, 0:1], [[1, 8], [0, 128]])
                S.op("dve", I("tensor_tensor",
                    HNd[:, :, c0:c0 + 128], pb.rearrange("p (c n) -> p c n", c=8), g_bc, ALU.mult),
                    reads=[("ps", b), "GPRE"], writes=[("HN", c0 // 128)])

        def postnorm(i, pair, GB, TMP, JUNK32):
            ps = pp[pair][:, :]
            S.op("act", I("activation", JUNK32, ps, AF.Square, accum_out=SS2[:, 0:1]),
                 reads=[("ps", 2 * pair), ("ps", 2 * pair + 1)], writes=["JUNK", "TMPN", "SS2"])
            S.op("act", I("activation", SS2[:, 1:2], SS2[:, 0:1], AF.Ln, scale=1.0 / D, bias=EPSC),
                 reads=["SS2", "EPSC"], writes=["SS2b"])
            S.op("act", I("activation", SS2[:, 2:3], SS2[:, 1:2], AF.Exp, scale=-0.5),
                 reads=["SS2b"], writes=["SS2c"])
            S.op("dve", I("scalar_tensor_tensor", TMP, ps, SS2[:, 2:3], GB, ALU.mult, ALU.mult),
                 reads=[("ps", 2 * pair), ("ps", 2 * pair + 1), "SS2c", "GB"], writes=["TMPN"])
            S.op("dve", I("tensor_tensor", X[:, i, :], X[:, i, :], TMP, ALU.add),
                 reads=["TMPN", ("X", i)], writes=[("X", i)])

        def load_gain_bc(GB, l, w):
            src = gains_d[l, w, :]
            src_bc = bass.AP(tensor=src.tensor, offset=src.offset, ap=[[0, 128], [1, D]])
            S.dma("sp", I("dma_start", out=GB, in_=src_bc), writes=["GB"])

        def outproj(CATv, w_d, l, GB, TMP, JUNK32, sb=8):
            for k in range(8):
                load_rowblock(w_d, k * 128, sb + k)
            load_gain_bc(GB, l, 1)
            for i in range(NT):
                pair = pair_ctr[0] % 4
                pair_ctr[0] += 1
                for half in range(2):
                    for k in range(8):
                        S.op("pe", I("matmul",
                            pp[pair][:, half * 512:(half + 1) * 512], lhsT=CATv(k)[:, i * 128:(i + 1) * 128],
                            rhs=WR[:, sb + k, half * 512:(half + 1) * 512], start=(k == 0), stop=(k == 7)),
                            reads=[("CAT", k, i // 4), ("wr", sb + k)], writes=[("ps", 2 * pair + half)])
                postnorm(i, pair, GB, TMP, JUNK32)


        def ffn(l):
            HN2 = V(HN0, BF16, [8, 1024])
            GTb = V(90112, BF16, [NFC, 1024])
            W2b = V(135168, BF16, [NFC, 1024])
            WRF = V(180224, BF16, [8, 1024])
            XNf = [V(196608, BF16, [D]), V(198656, BF16, [D])]
            SG = V(200704, F32, [512])
            TMPf = V(202752, F32, [D])
            JUNKf = V(202752, BF16, [D])
            GBf = V(206848, F32, [D])
            load_gain_bc(GBf, l, 3)
            JUNKs = V(200704, BF16, [D])
            cnt = 0
            prenorm(range(8), 1 + 2 * l, lambda i: i * 128, XNf, JUNKs, HNd=HN2, junk_keys=("SG",))
            for half in range(2):
                tiles = range(8 * half, 8 * half + 8)
                for fc in range(NFC):
                    slots = []
                    for wi, wd in enumerate((w1_d, w3_d)):
                        slot = cnt % 8
                        cnt += 1
                        slots.append(slot)
                        src = wd[l, :, fc * 128:(fc + 1) * 128].rearrange("(c p) n -> p c n", p=128)
                        dst = WRF[:, slot, :].rearrange("p (c n) -> p c n", c=8)
                        S.dma("pool", I("dma_start", out=dst, in_=src), writes=[("wrf", slot)])
                    if half == 0:
                        S.dma("pool", I("dma_start", out=W2b[:, fc, :], in_=w2_d[l, fc * 128:(fc + 1) * 128, :]), writes=[("w2", fc)])
                    for tb in range(2):
                        ba, bb = nbank(), nbank()
                        for wi, bnk in ((0, ba), (1, bb)):
                            wv = WRF[:, slots[wi], :].rearrange("p (c n) -> p c n", c=8)
                            for k in range(8):
                                S.op("pe", I("matmul", bank(bnk), lhsT=wv[:, k, :], rhs=HN2[:, k, tb * 512:(tb + 1) * 512],
                                             start=(k == 0), stop=(k == 7)),
                                     reads=[("wrf", slots[wi])] + [("HN", 4 * tb + j) for j in range(4)], writes=[("ps", bnk)])
                        S.op("act", I("activation", SG, bank(ba), AF.Silu), reads=[("ps", ba)], writes=["SG"])
                        S.op("dve", I("tensor_tensor", GTb[:, fc, tb * 512:(tb + 1) * 512], SG, bank(bb), ALU.mult),
                             reads=["SG", ("ps", bb)], writes=[("gt", fc, tb)])
                for i in tiles:
                    tl = i - 8 * half
                    pair = pair_ctr[0] % 3
                    pair_ctr[0] += 1
                    for hc in range(2):
                        for fc in range(NFC):
                            S.op("pe", I("matmul", pp[pair][:, hc * 512:(hc + 1) * 512], lhsT=GTb[:, fc, tl * 128:(tl + 1) * 128],
                                         rhs=W2b[:, fc, hc * 512:(hc + 1) * 512], start=(fc == 0), stop=(fc == NFC - 1)),
                                 reads=[("gt", fc, tl // 4), ("w2", fc)], writes=[("ps", 2 * pair + hc)])
                    if half == 0:
                        prenorm_tile(tl, 8 + tl, 1 + 2 * l, tl * 128, XNf, JUNKs, HN2, ("SG",), fixed_bank=6 + (tl % 2))
                    postnorm(i, pair, GBf, TMPf, TMPf)
                    if l == 1:
                        S.dma("sp", I("dma_start", out=y_d[i * 128:(i + 1) * 128, :], in_=X[:, i, :]), reads=[("X", i)])
            S.barrier()


        def attn_layer():
            OT = V(122880, BF16, [8, T])
            BIAS = V(155648, F32, [3, 8, 128])
            LOGS = [V(203840, F32, [512]), V(205888, F32, [512])]
            lcnt = [0]
            QT = V(167936, BF16, [4, T])
            KT = V(184320, BF16, [T])
            KT1 = V(199744, BF16, [T])
            VA = V(188416, BF16, [NT, 2, 65])
            PT = [V(192576, BF16, [3, 512]), V(195648, BF16, [3, 512]), V(207936, BF16, [3, 512])]
            OTK = [V(198720, BF16, [256]), V(199232, BF16, [256]), V(211008, BF16, [256])]
            scnt = [0]
            XNa = [V(199744, BF16, [D]), V(201792, BF16, [D])]
            JUNKa = V(203840, BF16, [D])
            GBa = V(155648, F32, [D])
            TMPa = V(167936, F32, [D])
            SCALE = 64.0 ** -0.5
            prenorm(range(NT), 2, lambda i: i * 128, XNa, JUNKa)
            S.op("dve", I("memset", VA[:, :, :, 64:65], 1.0), writes=["VA1"])
            S.barrier()
            S.op("dve", I("memset", KT[64:128, :], 0.0), writes=["KTZ"])
            S.op("dve", I("memset", KT1[0:64, :], 0.0), writes=["KTZ"])
            for pi in range(2):
                for g in range(4):
                    for hl in range(2):
                        hh = 8 * pi + 4 * hl + g
                        src = w_qkv_d[:, 64 * hh:64 * hh + 64].rearrange("(c p) n -> p c n", p=128)
                        dst = WR[:, g, :].rearrange("p (c n) -> p c n", c=8)[:, :, 64 * hl:64 * hl + 64]
                        S.dma("pool", I("dma_start", out=dst, in_=src), writes=[("wr", g)])
                load_colblock(w_qkv_d, 1024 + 128 * pi, 4)
                load_colblock(w_qkv_d, 1280 + 128 * pi, 5)
                S.dma("sp", I("dma_start", out=BIAS, in_=abias_d[:, :, 8 * pi:8 * pi + 8, :]), writes=["BIAS"])
                for g in range(5):
                    for tb in range(4):
                        b = nbank()
                        for k in range(8):
                            S.op("pe", I("matmul", bank(b), lhsT=wr_cols(g)[:, k, :], rhs=HN[:, k, tb * 512:(tb + 1) * 512],
                                         start=(k == 0), stop=(k == 7)),
                                 reads=[("wr", g)], writes=[("ps", b)])
                        if g < 4:
                            S.op("act", I("copy", QT[:, g, tb * 512:(tb + 1) * 512], bank(b)), reads=[("ps", b)],
                                 writes=[("QT", g, tb)])
                        else:
                            S.op("act", I("copy", KT[0:64, tb * 512:(tb + 1) * 512], bank(b)[0:64, :]), reads=[("ps", b), "KTZ"],
                                 writes=[("KT", tb)])
                            S.op("act", I("copy", KT1[64:128, tb * 512:(tb + 1) * 512], bank(b)[64:128, :]), reads=[("ps", b), "KTZ"],
                                 writes=[("KT1", tb)])
                for t4 in range(4):
                    b = nbank()
                    for tl in range(4):
                        ti = t4 * 4 + tl
                        for k in range(8):
                            S.op("pe", I("matmul", bank(b)[:, tl * 128:(tl + 1) * 128], lhsT=HN[:, k, ti * 128:(ti + 1) * 128],
                                         rhs=wr_cols(5)[:, k, :], start=(k == 0), stop=(k == 7)),
                                 reads=[("wr", 5)], writes=[("ps", b)])
                    S.op("act", I("copy", VA[:, t4 * 4:t4 * 4 + 4, :, 0:64],
                                  bank(b).rearrange("p (a b c) -> p a b c", a=4, b=2)),
                         reads=[("ps", b), "VA1"], writes=[("VA", t4)])
                iters = [(j, kl) for j in range(NT) for kl in range(2)]

                def stA(i):
                    j, kl = iters[i]
                    pb = 64 * kl
                    par = i % 3
                    for o in (-1, 0, 1):
                        if not (0 <= j + o < NT):
                            continue
                        oi = o + 1
                        jj = j + o
                        b = scnt[0] % 4
                        scnt[0] += 1
                        rhs = mkap(QT[:, 0, j * 128:j * 128 + 1], [[T, 4], [1, 128]])
                        ktz = KT if kl == 0 else KT1
                        S.op("pe", I("matmul", bank(b), lhsT=ktz[:, jj * 128:(jj + 1) * 128], rhs=rhs,
                                     start=True, stop=True),
                             reads=[("KT", jj // 4), ("KT1", jj // 4), "KTZ"] + [("QT", g, j // 4) for g in range(4)],
                             writes=[("ps", b)])
                        lg = LOGS[lcnt[0] % 2]
                        lk = ("LOG", lcnt[0] % 2)
                        lcnt[0] += 1
                        S.op("dve", I("tensor_tensor", lg, bank(b),
                                      BIAS[:, oi, 4 * kl:4 * kl + 4, :].rearrange("p a b -> p (a b)"), ALU.add),
                             reads=[("ps", b), "BIAS"], writes=[lk])
                        S.op("act", I("activation", PT[par][:, oi, :], lg, AF.Exp, scale=SCALE),
                             reads=[lk], writes=[("PT", par, oi)])

                def stB(i):
                    j, kl = iters[i]
                    kk = 2 * pi + kl
                    par = i % 3
                    offs = [o for o in (-1, 0, 1) if 0 <= j + o < NT]
                    bo = 4 + (i % 2)
                    for g in range(4):
                        for n_, o in enumerate(offs):
                            S.op("pe", I("matmul", bank(bo)[:, g * 65:(g + 1) * 65],
                                         lhsT=PT[par][:, o + 1, g * 128:(g + 1) * 128], rhs=VA[:, j + o, kl, :],
                                         start=(n_ == 0), stop=(n_ == len(offs) - 1)),
                                 reads=[("VA", (j + o) // 4), ("PT", par, o + 1)], writes=[("ps", bo)])
                    den = DEN3[:, 4 * par:4 * par + 4]
                    dps = mkap(bank(bo)[:, 64:65], [[65, 4]])
                    S.op("dve", I("tensor_tensor", den, dps, ESINK[:, 4 * kk:4 * kk + 4], ALU.add),
                         reads=[("ps", bo), "ESINK"], writes=[("DEN", par)])
                    S.op("dve", I("reciprocal", den, den), reads=[("DEN", par)], writes=[("DEN", par)])
                    ov = mkap(bank(bo)[:, 0:1], [[65, 4], [1, 64]])
                    dbc = mkap(den[:, 0:1], [[1, 4], [0, 64]])
                    S.op("dve", I("tensor_tensor", OTK[par].rearrange("p (a b) -> p a b", a=4), ov, dbc, ALU.mult),
                         reads=[("ps", bo), ("DEN", par)], writes=[("OTK", par)])

                def stC(i):
                    j, kl = iters[i]
                    kk = 2 * pi + kl
                    par = i % 3
                    bt = 6 + (i % 2)
                    ptb = bankb(bt)
                    for hp in range(2):
                        S.op("pe", I("transpose", ptb[:, hp * 128:(hp + 1) * 128], OTK[par][:, hp * 128:(hp + 1) * 128], IDENT),
                             reads=[("OTK", par), "IDENT"], writes=[("ps", bt)])
                    S.op("act", I("copy", OT[:, 2 * kk:2 * kk + 2, j * 128:(j + 1) * 128],
                                  ptb[:, 0:256].rearrange("p (a b) -> p a b", a=2)),
                         reads=[("ps", bt)], writes=[("CAT", 2 * kk, j // 4), ("CAT", 2 * kk + 1, j // 4)])

                nI = len(iters)
                for i in range(nI + 2):
                    if i < nI:
                        stA(i)
                    if 0 <= i - 1 < nI:
                        stB(i - 1)
                    if 0 <= i - 2 < nI:
                        stC(i - 2)
            S.barrier()
            outproj(lambda k: OT[:, k, :], w_co_d, 1, GBa, TMPa, TMPa, sb=0)
            S.barrier()

        OF = V(PH0, F32, [4, T])
        CATF = V(PH0 + 32768, BF16, [4, T])
        TM0 = PH0 + 49152

        def catv(k):
            if k < 4:
                return V(PH0 + 8192 * k + 4096, BF16, [T])
            return CATF[:, k - 4, :]
        XN = [V(TM0, BF16, [D]), V(TM0 + 2048, BF16, [D])]
        JUNK = V(TM0 + 4096, BF16, [D])
        prenorm(range(NT), 0, lambda i: i * 128, XN, JUNK)
        if dbg == "hn0":
            dump("hn0", HN, BF16, [8, T])
            dump("rstd", RSTD, F32, [16])
            dump("gpre", GPRE, F32, [4, 8])
            return fin()

        AB = V(PH0, BF16, [4, NT, 256])
        TABR = V(TM0, BF16, [4, 2, 512])
        UT = V(TM0 + 8192, BF16, [4, 512])
        for g in range(4):
            load_colblock(w_in_d, 2560 + g * 128, 12 + g)
        for bk in range(4):
            cs = slice(bk * 512, (bk + 1) * 512)
            for g in range(4):
                bu_ = nbank()
                ups = bank(bu_)
                for k in range(8):
                    S.op("pe", I("matmul", ups, lhsT=wr_cols(12 + g)[:, k, :], rhs=HN[:, k, cs],
                                                                     start=(k == 0), stop=(k == 7)),
                         reads=[("wr", 12 + g)] + [("HN", 4 * bk + j_) for j_ in range(4)], writes=[("ps", bu_)])
                S.op("act", I("copy", UT[:, g, :], ups), reads=[("ps", bu_)],
                     writes=[("UT", g), "ID32"] + [("STG", r_) for r_ in range(6)])
                ba = nbank()
                aps_ = bank(ba)
                ba2 = nbank()
                aps2 = bank(ba2)
                for tl in range(4):
                    tgt = (aps_ if tl < 2 else aps2)[:, (tl % 2) * 256:(tl % 2) * 256 + 256]
                    S.op("pe", I("matmul", tgt, lhsT=UT[:, g, tl * 128:(tl + 1) * 128], rhs=DFTC[:, 0:256], start=True, stop=True),
                         reads=[("UT", g), "DFTC"], writes=[("ps", ba if tl < 2 else ba2)])
                S.op("act", I("copy", AB[:, g, bk * 4:bk * 4 + 2, :].rearrange("p a b -> p (a b)"), aps_),
                     reads=[("ps", ba)], writes=[("AB", g, bk)])
                S.op("act", I("copy", AB[:, g, bk * 4 + 2:bk * 4 + 4, :].rearrange("p a b -> p (a b)"), aps2),
                     reads=[("ps", ba2)], writes=[("AB", g, bk)])
        S.barrier()
        tabv = dft_t_d.rearrange("(c p) s k -> p c s k", p=128)
        QS = [V(TM0 + 12288, F32, [512]), V(TM0 + 14336, F32, [512])]
        allcat = lambda g: [("CAT", 4 + g, b_) for b_ in range(4)]
        bmid = nbank()
        for g in range(4):
            for c in range(NT):
                S.op("pe", I("matmul", bank(bmid)[:, 2 * g:2 * g + 2], lhsT=AB[:, g, c, 0:128], rhs=DFTC[:, 256:258],
                             start=(c == 0), stop=(c == NT - 1)),
                     reads=[("AB", g, c // 4), "DFTC"], writes=[("ps", bmid)])
        S.op("act", I("copy", CATF[:, :, 1024:1025], mkap(bank(bmid)[:, 0:1], [[2, 4], [1, 1]])),
             reads=[("ps", bmid)], writes=[k_ for g in range(4) for k_ in allcat(g)])
        tcount = 0
        for kb in range(2):
            ks = slice(kb * 512, (kb + 1) * 512)
            for c in range(NT):
                slot = tcount % 4
                tcount += 1
                S.dma("sp", I("dma_start", out=TABR[:, slot, :, :], in_=tabv[:, c, :, ks]),
                      writes=[("tab", slot)])
                for g in range(4):
                    S.op("pe", I("matmul", bank(g), lhsT=AB[:, g, c, 0:128], rhs=TABR[:, slot, 0, :],
                                 start=(c == 0), stop=(c == NT - 1)),
                         reads=[("AB", g, c // 4), ("tab", slot)], writes=[("ps", g)])
                    S.op("pe", I("matmul", bank(4 + g), lhsT=AB[:, g, c, 128:256], rhs=TABR[:, slot, 1, :],
                                 start=(c == 0), stop=(c == NT - 1)),
                         reads=[("AB", g, c // 4), ("tab", slot)], writes=[("ps", 4 + g)])
            for g in range(4):
                qs = QS[g % 2]
                S.op("act", I("copy", qs, bank(4 + g)), reads=[("ps", 4 + g)], writes=[("QS", g % 2)])
                S.op("dve", I("tensor_tensor", CATF[:, g, ks], bank(g), qs, ALU.add),
                     reads=[("ps", g), ("QS", g % 2)], writes=allcat(g))
                if kb == 0:
                    mo = mkap(CATF[:, g, 2047:2048], [[-1, 511]])
                    S.op("dve", I("tensor_tensor", mo, bank(g)[:, 1:512], qs[:, 1:512], ALU.subtract),
                         reads=[("ps", g), ("QS", g % 2)], writes=allcat(g))
                else:
                    mo = mkap(CATF[:, g, 1536:1537], [[-1, 512]])
                    S.op("dve", I("tensor_tensor", mo, bank(g), qs, ALU.subtract),
                         reads=[("ps", g), ("QS", g % 2)], writes=allcat(g))
        if dbg == "fnet":
            dump("catf", CATF, BF16, [4, T])
            dump("ab", AB, BF16, [4, NT, 256])
            return fin()
        S.barrier()

        o_ = [TM0]

        def talloc(dt, shape):
            n = 1
            for s_ in shape:
                n *= s_
            nb = n * (4 if dt == F32 else 2)
            ap = V(o_[0], dt, shape)
            o_[0] += nb
            assert o_[0] <= ARENA_B, o_[0]
            return ap

        T1 = talloc(F32, [512])
        T2 = talloc(F32, [512])
        T3 = talloc(F32, [512])
        T4 = talloc(F32, [512])
        T6 = talloc(F32, [512])
        T7 = talloc(F32, [512])
        QA = talloc(BF16, [512])
        Q2 = talloc(BF16, [512])
        Q3 = talloc(BF16, [512])
        QTS = talloc(BF16, [512])
        KA = talloc(BF16, [512])
        KOUT = talloc(BF16, [512])
        KTET = talloc(BF16, [512])
        KTE = talloc(BF16, [4, 128])
        VT = talloc(BF16, [4, 128])
        PM = [talloc(BF16, [128]) for _ in range(4)]
        SBFL = [talloc(BF16, [128]) for _ in range(3)]

        QSC = 128.0 ** -0.5
        for h in range(4):
            load_colblock(w_in_d, 0 + h * 128, 0 + h)
            load_colblock(w_in_d, 512 + h * 128, 4 + h)
            load_colblock(w_in_d, 1536 + h * 128, 8 + h)

        def rev512(ap2d):
            base = ap2d[:, 511:512]
            return mkap(base, [[-1, 512]])

        def hgrn_sweep(dr):
            fwd = (dr == 0)
            blocks = range(4) if fwd else range(3, -1, -1)
            for h in range(4):
                S.op("dve", I("memset", S32[:, h, :], 0.0), reads=[("SBF", h)], writes=[("S32", h)])
            first_tile = 0 if fwd else NT - 1
            for bk in blocks:
                cs = slice(bk * 512, (bk + 1) * 512)
                for h in range(4):
                    bz, bq, bv = 0, 1, 2
                    zps, qps, vps = bank(bz), bank(bq), bank(bv)
                    for k in range(8):
                        S.op("pe", I("matmul", zps, lhsT=wr_cols(4 + h)[:, k, :], rhs=HN[:, k, cs],
                                                                         start=(k == 0), stop=(k == 7)),
                             reads=[("wr", 4 + h)], writes=[("ps", bz)])
                    for k in range(8):
                        S.op("pe", I("matmul", qps, lhsT=wr_cols(h)[:, k, :], rhs=HN[:, k, cs],
                                                                         start=(k == 0), stop=(k == 7)),
                             reads=[("wr", h)], writes=[("ps", bq)])
                    for tl in range(4):
                        ti = bk * 4 + tl
                        for k in range(8):
                            S.op("pe", I("matmul",
                                vps[:, tl * 128:(tl + 1) * 128], lhsT=HN[:, k, ti * 128:(ti + 1) * 128],
                                rhs=wr_cols(8 + h)[:, k, :], start=(k == 0), stop=(k == 7)),
                                reads=[("wr", 8 + h)], writes=[("ps", bv)])
                    bt = 3
                    ptb = bankb(bt)

                    def chain(hb):
                        sl = slice(256 * hb, 256 * hb + 256)
                        K = lambda n: (n, hb)
                        t1, t2, t3, t4, t6, t7 = (x[:, sl] for x in (T1, T2, T3, T4, T6, T7))
                        qa, q2, q3, qts, ka, kout, ktet = (x[:, sl] for x in (QA, Q2, Q3, QTS, KA, KOUT, KTET))
                        zp, qp = zps[:, sl], qps[:, sl]
                        S.op("act", I("activation", t1, zp, AF.Exp, scale=-1.0), reads=[("ps", bz)], writes=[K("T1")])
                        yield
                        S.op("act", I("activation", t7, t1, AF.Ln, bias=ONEC), reads=[K("T1"), "ONEC"], writes=[K("T7")])
                        yield
                        S.op("act", I("activation", t2, t1, AF.Ln, scale=LB[:, dr, h:h + 1], bias=ONEC),
                             reads=[K("T1"), "ONEC", "LB"], writes=[K("T2")])
                        yield
                        S.op("dve", I("tensor_tensor", t2, t2, t7, ALU.subtract), reads=[K("T2"), K("T7")], writes=[K("T2")])
                        yield
                        S.op("dve", I("scalar_tensor_tensor", t1, zp, -1.0, t7, ALU.mult, ALU.subtract),
                             reads=[("ps", bz), K("T7"), K("T1")], writes=[K("T1")])
                        yield
                        if fwd:
                            g_o, g_i, gt_o = t3, t2, t4
                        else:
                            rv = lambda x: mkap(x[:, 255:256], [[-1, 256]])
                            g_o, g_i, gt_o = rv(t3), rv(t2), rv(t4)
                        S.op("dve", I("tensor_tensor_scan", g_o, RMASK[:, 0, 0:256], g_i, 0.0, ALU.mult, ALU.add),
                             reads=[K("T2"), "RMASK"], writes=[K("T3")])
                        yield
                        S.op("dve", I("tensor_tensor_scan", gt_o, RMASK[:, 1, 0:256], g_i, 0.0, ALU.mult, ALU.add),
                             reads=[K("T2"), "RMASK"], writes=[K("T4")])
                        yield
                        S.op("act", I("activation", t2, t3, AF.Exp), reads=[K("T3")], writes=[K("T2")])
                        yield
                        S.op("act", I("activation", t6, t4, AF.Exp), reads=[K("T4")], writes=[K("T6")])
                        yield
                        S.op("dve", I("tensor_tensor", t3, t1, t3, ALU.subtract), reads=[K("T1"), K("T3"), K("T2")], writes=[K("T3")])
                        yield
                        S.op("act", I("activation", ka, t3, AF.Exp, bias=LNOML[:, dr, h:h + 1]),
                             reads=[K("T3"), "LNOML"], writes=[K("KA")])
                        yield
                        lt_pos = 127 if fwd else 0
                        lt_bc = mkap(t4[:, lt_pos:lt_pos + 1], [[128, 2], [0, 128]])
                        S.op("dve", I("tensor_tensor", t7, t1, t4, ALU.subtract), reads=[K("T1"), K("T4"), K("T7")], writes=[K("T7")])
                        yield
                        S.op("dve", I("tensor_tensor",
                            t7.rearrange("p (a b) -> p a b", a=2), t7.rearrange("p (a b) -> p a b", a=2), lt_bc, ALU.add),
                            reads=[K("T4"), K("T7")], writes=[K("T7")])
                        yield
                        S.op("act", I("activation", ktet, t7, AF.Exp, bias=LNOML[:, dr, h:h + 1]),
                             reads=[K("T7"), "LNOML"], writes=[K("KTET")])
                        yield
                        S.op("dve", I("scalar_tensor_tensor", qa, qp, QSC, t2, ALU.mult, ALU.mult),
                             reads=[("ps", bq), K("T2")], writes=[K("QA")])
                        yield
                        S.op("dve", I("scalar_tensor_tensor", qts, qp, QSC, t6, ALU.mult, ALU.mult),
                             reads=[("ps", bq), K("T6")], writes=[K("QTS")])
                        yield
                        dpos = 31 if fwd else 0
                        if fwd:
                            d1 = mkap(t2[:, dpos:dpos + 1], [[32, 7], [0, 32]])
                            d2 = mkap(t2[:, dpos:dpos + 1], [[32, 6], [0, 32]])
                            q2o, q2i = q2[:, 32:256], qa[:, 32:256]
                            q3o, q3i = q3[:, 64:256], q2[:, 64:256]
                        else:
                            d1 = mkap(t2[:, 32:33], [[32, 7], [0, 32]])
                            d2 = mkap(t2[:, 64:65], [[32, 6], [0, 32]])
                            q2o, q2i = q2[:, 0:224], qa[:, 0:224]
                            q3o, q3i = q3[:, 0:192], q2[:, 0:192]
                        S.op("dve", I("tensor_tensor",
                            q2o.rearrange("p (a b) -> p a b", b=32), q2i.rearrange("p (a b) -> p a b", b=32), d1, ALU.mult),
                            reads=[K("QA"), K("T2")], writes=[K("Q2")])
                        yield
                        S.op("dve", I("tensor_tensor",
                            q3o.rearrange("p (a b) -> p a b", b=32), q3i.rearrange("p (a b) -> p a b", b=32), d2, ALU.mult),
                            reads=[K("Q2"), K("T2")], writes=[K("Q3")])
                        yield
                        down = mkap(t2[:, dpos:dpos + 1], [[32, 8], [0, 32]])
                        S.op("dve", I("tensor_tensor",
                            kout.rearrange("p (a b) -> p a b", b=32), ka.rearrange("p (a b) -> p a b", b=32), down, ALU.mult),
                            reads=[K("KA"), K("T2")], writes=[K("KOUT")])
                        yield
                        for t2l in range(2):
                            tl_ = 2 * hb + t2l
                            S.op("pe", I("transpose", ptb[:, tl_ * 128:(tl_ + 1) * 128], KTET[:, tl_ * 128:(tl_ + 1) * 128], IDENT),
                                 reads=[K("KTET"), "IDENT"], writes=[("ps", bt)])
                        S.op("act", I("copy", KTE[:, 2 * hb:2 * hb + 2, :].rearrange("p a b -> p (a b)"), ptb[:, sl]),
                             reads=[("ps", bt)], writes=[K("KTE")])
                        yield

                    gens = [chain(hb_) for hb_ in ((0, 1) if fwd else (1, 0))]
                    while gens:
                        for g_ in list(gens):
                            try:
                                next(g_)
                            except StopIteration:
                                gens.remove(g_)
                    S.op("act", I("copy", VT.rearrange("p a b -> p (a b)"), vps),
                         reads=[("ps", bv)], writes=["VT"])
                    bs, bo, bu = 4, 5, 6
                    sps, ops_, ups = bank(bs), bank(bo), bank(bu)
                    tls = list(range(4)) if fwd else list(range(3, -1, -1))
                    mk = HMASK[:, 0:128] if fwd else HMASK[:, 128:256]
                    for tl in tls:
                        c0 = tl * 128
                        hb_t = tl // 2
                        sc_ = sps[:, c0:c0 + 128]
                        S.op("pe", I("matmul", sc_, lhsT=KA[:, c0:c0 + 128], rhs=QA[:, c0:c0 + 128], start=True, stop=True),
                             reads=[("KA", hb_t), ("QA", hb_t)], writes=[("ps", bs)])
                        for dd, Qd in ((1, QA), (2, Q2), (3, Q3)):
                            for a in range(4):
                                b_ = a - dd if fwd else a + dd
                                if b_ < 0 or b_ > 3:
                                    continue
                                kw = {}
                                if b_ == 3:
                                    kw["tile_position"] = (0, 96)
                                S.op("pe", I("matmul",
                                    sps[32 * b_:32 * b_ + 32, c0 + 32 * a:c0 + 32 * a + 32],
                                    lhsT=KOUT[:, c0 + 32 * b_:c0 + 32 * b_ + 32], rhs=Qd[:, c0 + 32 * a:c0 + 32 * a + 32],
                                    start=True, stop=True, **kw),
                                    reads=[("KOUT", hb_t), ("QA", hb_t), ("Q2", hb_t), ("Q3", hb_t)], writes=[("ps", bs)])
                    for tl in tls:
                        c0 = tl * 128
                        S.op("dve", I("tensor_tensor", PM[tl], sps[:, c0:c0 + 128], mk, ALU.mult),
                             reads=[("ps", bs), "HMASK"], writes=[("PM", tl)])
                    for tl in tls:
                        c0 = tl * 128
                        S.op("pe", I("matmul", ups[:, c0:c0 + 128], lhsT=KTE[:, tl, :], rhs=VT[:, tl, :], start=True, stop=True,
                                     skip_group_check=True),
                             reads=[("KTE", tl // 2), "VT"], writes=[("ps", bu)])
                    for n_, tl in enumerate(tls):
                        c0 = tl * 128
                        S.op("pe", I("matmul", ops_[:, c0:c0 + 128], lhsT=VT[:, tl, :], rhs=PM[tl], start=(n_ == 0), stop=True,
                                     skip_group_check=True),
                             reads=["VT", ("PM", tl)], writes=[("ps", bo)])
                    for n_, tl in enumerate(tls):
                        c0 = tl * 128
                        ti = bk * 4 + tl
                        if ti != first_tile:
                            sprev = SBF[:, h, :] if n_ == 0 else SBFL[n_ - 1]
                            skey = ("SBF", h) if n_ == 0 else ("SBFL", n_ - 1)
                            S.op("pe", I("matmul", ops_[:, c0:c0 + 128], lhsT=sprev, rhs=QTS[:, c0:c0 + 128], start=False, stop=True,
                                         skip_group_check=True),
                                 reads=[skey, ("QTS", tl // 2)], writes=[("ps", bo)])
                        dtp = (c0 + 127) if fwd else c0
                        S.op("dve", I("scalar_tensor_tensor",
                            S32[:, h, :], S32[:, h, :], T6[:, dtp:dtp + 1], ups[:, c0:c0 + 128], ALU.mult, ALU.add),
                            reads=[("S32", h), ("T6", tl // 2), ("ps", bu)], writes=[("S32", h)])
                        if n_ < 3:
                            S.op("act", I("copy", SBFL[n_], S32[:, h, :]), reads=[("S32", h)], writes=[("SBFL", n_)])
                        else:
                            S.op("act", I("copy", SBF[:, h, :], S32[:, h, :]), reads=[("S32", h)], writes=[("SBF", h)])
                    if fwd:
                        S.op("act", I("copy", OF[:, h, cs], ops_),
                             reads=[("ps", bo)], writes=[("OF", h, bk)])
                    else:
                        S.op("dve", I("tensor_tensor", T1, ops_, OF[:, h, cs], ALU.add),
                             reads=[("ps", bo), ("OF", h, bk)], writes=[("T1", 0), ("T1", 1)])
                        S.op("act", I("activation", Q2, T1, AF.Square), reads=[("T1", 0), ("T1", 1)], writes=[("Q2", 0), ("Q2", 1)])
                        bn = 7
                        nps = bank(bn)
                        S.op("pe", I("matmul", nps, lhsT=ONES, rhs=Q2, start=True, stop=True),
                             reads=["ONES", ("Q2", 0), ("Q2", 1)], writes=[("ps", bn)])
                        S.op("act", I("activation", T2, nps, AF.Ln, scale=1.0 / 128, bias=EPSC),
                             reads=[("ps", bn), "EPSC"], writes=[("T2", 0), ("T2", 1)])
                        S.op("act", I("activation", T2, T2, AF.Exp, scale=-0.5), reads=[("T2", 0), ("T2", 1)], writes=[("T2", 0), ("T2", 1)])
                        S.op("dve", I("scalar_tensor_tensor", T3, T1, OGAIN[:, h:h + 1], T2, ALU.mult, ALU.mult),
                             reads=[("T1", 0), ("T1", 1), ("T2", 0), ("T2", 1), "OGAIN"], writes=[("T3", 0), ("T3", 1)])
                        bg = 3
                        gps = bank(bg)
                        for k in range(8):
                            S.op("pe", I("matmul", gps, lhsT=wr_cols(12 + h)[:, k, :], rhs=HN[:, k, cs],
                                                                             start=(k == 0), stop=(k == 7)),
                                 reads=[("wr", 12 + h)], writes=[("ps", bg)])
                        S.op("act", I("activation", T4, gps, AF.Exp, scale=-1.0),
                             reads=[("ps", bg)], writes=[("T4", 0), ("T4", 1)])
                        S.op("act", I("activation", T4, T4, AF.Ln, bias=ONEC), reads=[("T4", 0), ("T4", 1), "ONEC"], writes=[("T4", 0), ("T4", 1)])
                        S.op("act", I("activation", T4, T4, AF.Exp, scale=-1.0), reads=[("T4", 0), ("T4", 1)], writes=[("T4", 0), ("T4", 1)])
                        S.op("dve", I("tensor_tensor", T4, gps, T4, ALU.mult), reads=[("T4", 0), ("T4", 1), ("ps", bg)], writes=[("T4", 0), ("T4", 1)])
                        S.op("dve", I("tensor_tensor", catv(h)[:, cs], T3, T4, ALU.mult),
                             reads=[("T3", 0), ("T3", 1), ("T4", 0), ("T4", 1)], writes=[("CAT", h, bk)])

        hgrn_sweep(0)
        for h in range(4):
            load_colblock(w_in_d, 1024 + h * 128, 4 + h)
            load_colblock(w_in_d, 2048 + h * 128, 12 + h)
        hgrn_sweep(1)
        S.barrier()

        GB = V(TM0 + 12288, F32, [D])
        TMPN = V(TM0 + 16384, F32, [D])
        JUNK32 = V(TM0 + 20480, BF16, [D])
        outproj(catv, w_abo_d, 0, GB, TMPN, JUNK32)
        S.barrier()

        if stage >= 2:
            ffn(0)
        if stage >= 3:
            attn_layer()
        if stage >= 4:
            ffn(1)
        S.barrier()
        if stage < 4:
            for i in range(NT):
                S.dma("sp", I("dma_start", out=y_d[i * 128:(i + 1) * 128, :], in_=X[:, i, :]),
                      reads=[("X", i)])
        S.emit(nc)
    nc._sched_stats = S.stats
    return nc


_NAMES = ["x", "norm_gains", "ab_w_in", "ab_lb_table", "ab_out_gain", "ab_w_out", "c_w_qkv", "c_sink",
          "c_w_out", "ffn_w1", "ffn_w3", "ffn_w2"]


def make_in_maps(inputs):
    c = _consts()
    f = lambda a: np.ascontiguousarray(np.asarray(a, dtype=np.float32))
    shared = {
        "norm_gains": f(inputs["norm_gains"]),
        "ab_w_in": f(inputs["ab_w_in"])[0],
        "ab_lb_table": f(inputs["ab_lb_table"]),
        "ab_out_gain": f(inputs["ab_out_gain"])[0],
        "ab_w_out": f(inputs["ab_w_out"])[0],
        "c_w_qkv": f(inputs["c_w_qkv"])[0],
        "c_sink": f(inputs["c_sink"]),
        "c_w_out": f(inputs["c_w_out"])[0],
        "ffn_w1": f(inputs["ffn_w1"]),
        "ffn_w3": f(inputs["ffn_w3"]),
        "ffn_w2": f(inputs["ffn_w2"]),
        "k_ident": c["ident"], "k_ident32": c["ident32"], "k_dft_t": c["dft_t"], "k_dft_c": c["dft_c"],
        "k_hmask": c["hmask"], "k_rmask": c["rmask"], "k_abias": c["abias"],
    }
    x = f(inputs["x"])
    return [dict(shared, x=x[b]) for b in range(x.shape[0])]


def kernel(**inputs):
    nc = build()
    in_maps = make_in_maps(inputs)
    res = run_bass_kernel_spmd(nc, in_maps, core_ids=list(range(8)))
    return np.stack([np.asarray(r["y"], dtype=np.float32) for r in res.results], axis=0)
```

```python
import math
from contextlib import ExitStack

import numpy as np
import ml_dtypes
import concourse.bass as bass
import concourse.mybir as mybir
from concourse.bass_utils import run_bass_kernel_spmd

F32 = mybir.dt.float32
BF16 = mybir.dt.bfloat16
AF = mybir.ActivationFunctionType
ALU = mybir.AluOpType

T = 2048
D = 1024
NT = 16
DFF = 2816
NFC = 22
EPS = 1e-6

ENGS = ("pe", "act", "dve", "pool", "sp")
NDMA_SLOTS = 32


class Op:
    __slots__ = ("eng", "fn", "deps", "is_dma", "slot", "dval", "needed", "sval", "idx")

    def __init__(self, eng, fn, is_dma):
        self.eng = eng
        self.fn = fn
        self.deps = []
        self.is_dma = is_dma
        self.slot = None
        self.dval = None
        self.needed = False
        self.sval = None


class Sched:
    def __init__(self):
        self.ops = {e: [] for e in ENGS}
        self.last_w = {}
        self.readers = {}
        self.ndma = 0
        self.slot_last = [None] * NDMA_SLOTS
        self.slot_cnt = [0] * NDMA_SLOTS
        self.bar = []
        self.pool_dmas = []
        self.nq = [0, 0]

    def barrier(self):
        b = []
        for e in ENGS:
            for op in reversed(self.ops[e]):
                if not op.is_dma:
                    b.append(op)
                    break
        for s in range(NDMA_SLOTS):
            if self.slot_last[s] is not None:
                b.append(self.slot_last[s])
        self.bar = b

    def _add(self, eng, fn, reads, writes, is_dma, nobar=False):
        op = Op(eng, fn, is_dma)
        deps = []
        for r in reads:
            w = self.last_w.get(r)
            if w is not None:
                deps.append(w)
        for w_ in writes:
            w = self.last_w.get(w_)
            if w is not None:
                deps.append(w)
            deps.extend(self.readers.get(w_, ()))
        if not nobar:
            deps.extend(self.bar)
        seen = set()
        latest = {}
        for d in deps:
            if d is op or id(d) in seen:
                continue
            seen.add(id(d))
            if d.is_dma:
                op.deps.append(d)
                continue
            if (not is_dma) and d.eng == "pe" and eng == "pe":
                continue
            cur = latest.get(d.eng)
            if cur is None or d.idx > cur.idx:
                latest[d.eng] = d
        for d in latest.values():
            op.deps.append(d)
            d.needed = True
        for r in reads:
            self.readers.setdefault(r, []).append(op)
        for w_ in writes:
            self.last_w[w_] = op
            self.readers[w_] = []
        if is_dma and eng == "pool":
            self.pool_dmas.append(op)
            if len(self.pool_dmas) > 4:
                pd = self.pool_dmas[-5]
                if id(pd) not in seen:
                    seen.add(id(pd))
                    op.deps.append(pd)
        if is_dma:
            half = NDMA_SLOTS // 2
            qi = 1 if eng == "pool" else 0
            s = qi * half + self.nq[qi] % half
            self.nq[qi] += 1
            self.ndma += 1
            prev = self.slot_last[s]
            if prev is not None and id(prev) not in seen:
                op.deps.append(prev)
            self.slot_cnt[s] += 16
            op.slot = s
            op.dval = self.slot_cnt[s]
            self.slot_last[s] = op
        op.idx = len(self.ops[eng])
        self.ops[eng].append(op)
        return op

    def op(self, eng, fn, reads=(), writes=()):
        return self._add(eng, fn, reads, writes, False)

    def dma(self, eng, fn, reads=(), writes=(), nobar=False):
        return self._add(eng, fn, reads, writes, True, nobar)

    def emit(self, nc):
        with ExitStack() as es:
            esem = {e: es.enter_context(nc.semaphore("s_" + e)) for e in ENGS if e != "sp"}
            dsem = [es.enter_context(nc.semaphore("d%d" % i)) for i in range(NDMA_SLOTS)]
            for e in ENGS:
                c = 0
                for op in self.ops[e]:
                    if op.is_dma:
                        continue
                    if op.needed:
                        c += 1
                        op.sval = c
            self.stats = {e: (len(self.ops[e]), sum(1 for o in self.ops[e] if o.needed and not o.is_dma)) for e in ENGS}
            block = es.enter_context(nc.Block())
            hw = {"pe": block.tensor, "act": block.scalar, "dve": block.vector,
                  "pool": block.gpsimd, "sp": block.sync}

            def run(ename):
                ops = self.ops[ename]

                def body(eng):
                    known = {}
                    for op in ops:
                        for d in op.deps:
                            if d.is_dma:
                                key = ("d", d.slot)
                                sem = dsem[d.slot]
                                val = d.dval
                            else:
                                key = ("e", d.eng)
                                sem = esem[d.eng]
                                val = d.sval
                            if known.get(key, 0) >= val:
                                continue
                            eng.wait_ge(sem, val)
                            known[key] = val
                        if callable(op.fn):
                            ins = op.fn(eng)
                        else:
                            ins = getattr(eng, op.fn[0])(*op.fn[1], **op.fn[2])
                        if op.is_dma:
                            ins.then_inc(dsem[op.slot], 16)
                        elif op.needed:
                            ins.then_inc(esem[ename], 1)
                    if ename == "sp":
                        for s in range(NDMA_SLOTS):
                            if self.slot_cnt[s] > 0:
                                eng.wait_ge(dsem[s], self.slot_cnt[s])

                hw[ename](body)

            for e in ENGS:
                run(e)


def I(name, *args, **kw):
    return (name, args, kw)


def mkap(base, dims):
    return bass.AP(tensor=base.tensor, offset=base.offset,
                   ap=[list(base.ap[0])] + [list(d) for d in dims])


_CONSTS = None


def _consts():
    global _CONSTS
    if _CONSTS is not None:
        return _CONSTS
    bf = ml_dtypes.bfloat16
    c = {}
    c["ident"] = np.eye(128, dtype=np.float32).astype(bf)
    c["ident32"] = np.eye(128, dtype=np.float32)
    t = np.arange(T, dtype=np.int64)
    ph = (np.outer(t, t) % T).astype(np.float64) * (2.0 * np.pi / T)
    tab = np.stack([np.cos(ph), np.sin(ph)], axis=1) / math.sqrt(T)
    c["dft_t"] = np.ascontiguousarray(tab.astype(np.float32).astype(bf))
    cc = np.arange(128, dtype=np.int64)
    phc = (np.outer(cc, cc) % 128).astype(np.float64) * (2.0 * np.pi / 128)
    ccsc = np.concatenate([np.cos(phc), -np.sin(phc)], axis=1) / math.sqrt(128.0)
    alt = np.where(np.arange(128) % 2 == 0, 1.0, -1.0)[:, None] / math.sqrt(T)
    ccsc = np.concatenate([ccsc, alt, np.zeros((128, 1))], axis=1)
    c["dft_c"] = ccsc.astype(np.float32).astype(bf)
    s = np.arange(128)[:, None]
    q = np.arange(128)[None, :]
    mf = (s <= q).astype(np.float32)
    mb = (s >= q).astype(np.float32)
    c["hmask"] = np.concatenate([mf, mb], axis=1).astype(np.float32).astype(bf)
    j = np.arange(512)
    rm = np.stack([(j % 32 != 0), (j % 128 != 0)], axis=0).astype(np.float32)
    c["rmask"] = np.ascontiguousarray(np.broadcast_to(rm[None], (128, 2, 512))).astype(np.float32).astype(bf)
    slopes = 2.0 ** (-8.0 * np.arange(1, 17, dtype=np.float64) / 16.0)
    scale = 64.0 ** -0.5
    bias = np.zeros((128, 3, 16, 128), dtype=np.float64)
    for oi, off in enumerate((-1, 0, 1)):
        dist = np.abs(q - (s + 128 * off)).astype(np.float64)
        valid = dist <= 128
        for h in range(16):
            b = -slopes[h] * dist / scale
            bias[:, oi, h, :] = np.where(valid, b, -1.0e9)
    bhi = bias.astype(np.float32).astype(bf)
    blo = (bias - bhi.astype(np.float64)).astype(np.float32).astype(bf)
    c["abias_hi"] = bhi
    c["abias_lo"] = blo
    _CONSTS = c
    return c


def build(stage=99, dbg=False):
    nc = bass.Bass("TRN2", target_bir_lowering=False)

    def din(name, shape, dt=F32):
        return nc.dram_tensor(name, list(shape), dt, kind="ExternalInput").ap()

    x_d = din("x", [T, D])
    gains_d = din("norm_gains", [2, 4, D])
    w_in_d = din("ab_w_in", [D, 3072])
    lb_d = din("ab_lb_table", [2, 2, 512])
    og_d = din("ab_out_gain", [4, 128])
    w_abo_d = din("ab_w_out", [D, D])
    w_qkv_d = din("c_w_qkv", [D, 1536])
    sink_d = din("c_sink", [1, 16])
    w_co_d = din("c_w_out", [D, D])
    w1_d = din("ffn_w1", [2, D, DFF])
    w3_d = din("ffn_w3", [2, D, DFF])
    w2_d = din("ffn_w2", [2, DFF, D])
    ident_d = din("k_ident", [128, 128], BF16)
    ident32_d = din("k_ident32", [128, 128])
    dft_t_d = din("k_dft_t", [T, 2, T], BF16)
    dft_c_d = din("k_dft_c", [128, 258], BF16)
    hmask_d = din("k_hmask", [128, 256], BF16)
    rmask_d = din("k_rmask", [128, 2, 512], BF16)
    abias_hi_d = din("k_abias_hi", [128, 3, 16, 128], BF16)
    abias_lo_d = din("k_abias_lo", [128, 3, 16, 128], BF16)
    y_d = nc.dram_tensor("y", [T, D], F32, kind="ExternalOutput").ap()

    S = Sched()
    with ExitStack() as es:
        ARENA_B = 212800
        arena = es.enter_context(nc.sbuf_tensor("arena", [128, ARENA_B // 4], F32))
        pp = [es.enter_context(nc.psum_tensor("pp%d" % i, [128, 1024], F32)) for i in range(4)]

        def V(off, dt, shape):
            n = 1
            for s_ in shape:
                n *= s_
            assert off % 4 == 0
            if dt == F32:
                ap = arena[:, off // 4: off // 4 + n]
            else:
                assert n % 2 == 0
                ap = arena[:, off // 4: off // 4 + n // 2].bitcast(BF16)
            if len(shape) == 2:
                ap = ap.rearrange("p (a b) -> p a b", a=shape[0])
            elif len(shape) == 3:
                ap = ap.rearrange("p (a b c) -> p a b c", a=shape[0], b=shape[1])
            return ap

        def bank(i):
            return pp[i // 2][:, (i % 2) * 512:(i % 2) * 512 + 512]

        def bankb(i):
            return pp[i // 2][:, (i % 2) * 512:(i % 2) * 512 + 512].bitcast(BF16)

        bank_ctr = [0]
        pair_ctr = [0]

        def nbank():
            b = bank_ctr[0] % 8
            bank_ctr[0] += 1
            return b

        X0 = 0
        C0 = 65536
        HN0 = 73728
        WR0 = 106496
        PH0 = 139264
        X = V(X0, F32, [NT, D])
        HN = V(HN0, BF16, [8, T])
        WR = V(WR0, BF16, [16, 1024])
        co = [C0]

        def calloc(dt, shape):
            n = 1
            for s_ in shape:
                n *= s_
            nb = n * (4 if dt == F32 else 2)
            nb = (nb + 3) // 4 * 4
            ap = V(co[0], dt, shape)
            co[0] += nb
            assert co[0] <= HN0
            return ap

        IDENT = calloc(BF16, [128])
        ONES = calloc(BF16, [128])
        DFTC = calloc(BF16, [258])
        HMASK = calloc(BF16, [256])
        RMASK = calloc(BF16, [2, 512])
        CST = calloc(F32, [52])
        GPRE = CST[:, 0:32].rearrange("p (a b) -> p a b", a=4)
        LBT = CST[:, 32:48].rearrange("p (a b c) -> p a b c", a=2, b=2)
        LB = calloc(F32, [2, 4])
        OML = calloc(F32, [2, 4])
        OGAIN = CST[:, 48:52]
        ESINK = calloc(F32, [16])
        EPSC = calloc(F32, [1])
        ONEC = calloc(F32, [1])
        LNOML = calloc(F32, [2, 4])
        SS = calloc(F32, [16])
        LNV = calloc(F32, [16])
        RSTD = calloc(F32, [16])
        SS2 = calloc(F32, [4])
        S32 = calloc(F32, [4, 128])
        SBF = calloc(BF16, [4, 128])
        DEN = calloc(F32, [8])
        DEN3 = calloc(F32, [12])

        def wr_cols(slot):
            return WR[:, slot, :].rearrange("p (c n) -> p c n", c=8)

        def load_colblock(w2d, col0, slot, ncols=128):
            src = w2d[:, col0:col0 + ncols].rearrange("(c p) n -> p c n", p=128)
            dst = WR[:, slot, 0:8 * ncols].rearrange("p (c n) -> p c n", c=8)
            S.dma("pool", I("dma_start", out=dst, in_=src), writes=[("wr", slot)], nobar=True)

        def load_rowblock(w2d, row0, slot):
            src = w2d[row0:row0 + 128, :]
            S.dma("pool", I("dma_start", out=WR[:, slot, :], in_=src), writes=[("wr", slot)], nobar=True)

        def dump(name, ap, dt, shape):
            S.barrier()
            d = nc.dram_tensor("dbg_" + name, [128] + list(shape), dt, kind="ExternalOutput").ap()
            S.dma("sp", I("dma_start", out=d, in_=ap))

        def fin():
            S.barrier()
            S.emit(nc)
            return nc

        for i in range(NT):
            S.dma("sp", I("dma_start", out=X[:, i, :], in_=x_d[i * 128:(i + 1) * 128, :]),
                  writes=[("X", i)])
        S.dma("sp", I("dma_start", out=IDENT, in_=ident_d), writes=["IDENT"])
        S.dma("sp", I("dma_start", out=DFTC, in_=dft_c_d), writes=["DFTC"])
        S.dma("sp", I("dma_start", out=HMASK, in_=hmask_d), writes=["HMASK"])
        S.dma("sp", I("dma_start", out=RMASK, in_=rmask_d), writes=["RMASK"])
        STG = V(PH0 + 49152 + 8192, F32, [128])[0:52, :]
        ID32 = V(PH0 + 49152 + 8704, F32, [128])
        S.dma("sp", I("dma_start", out=ID32, in_=ident32_d), writes=["ID32"])
        for r, (l, w) in enumerate(((0, 0), (0, 2), (1, 0), (1, 2))):
            S.dma("sp", I("dma_start", out=STG[8 * r:8 * r + 8, :], in_=gains_d[l, w, :].rearrange("(c p) -> c p", p=128)),
                  writes=[("STG", r)])
        S.dma("sp", I("dma_start", out=STG[32:48, :], in_=lb_d.rearrange("e d (h k) -> (e d h) k", k=128)), writes=[("STG", 4)])
        S.dma("sp", I("dma_start", out=STG[48:52, :], in_=og_d), writes=[("STG", 5)])
        bst = nbank()
        S.op("pe", I("transpose", bank(bst)[:, 0:52], STG, ID32[0:52, 0:52]),
             reads=[("STG", r) for r in range(6)] + ["ID32"], writes=[("ps", bst)])
        S.op("act", I("copy", CST, bank(bst)[:, 0:52]), reads=[("ps", bst)], writes=["GPRE", "LBT", "OGAIN"])
        sink_bc = bass.AP(tensor=sink_d.tensor, offset=sink_d.offset, ap=[[0, 128], [1, 16]])
        S.dma("sp", I("dma_start", out=ESINK, in_=sink_bc), writes=["ESINK"])
        S.op("pool", I("memset", ONES, 1.0), writes=["ONES"])
        S.op("pool", I("memset", EPSC, EPS), writes=["EPSC"])
        S.op("pool", I("memset", ONEC, 1.0), writes=["ONEC"])
        S.op("dve", I("tensor_tensor", LB, LBT[:, 1, :, :], LBT[:, 0, :, :], ALU.subtract),
             reads=["LBT"], writes=["LB"])
        S.op("act", I("activation", LB, LB, AF.Exp), reads=["LB"], writes=["LB"])
        S.op("dve", I("tensor_scalar", LB, LB, 1.0, None, ALU.add), reads=["LB"], writes=["LB"])
        S.op("dve", I("reciprocal", LB, LB), reads=["LB"], writes=["LB"])
        S.op("dve", I("tensor_scalar", OML, LB, -1.0, 1.0, ALU.mult, ALU.add), reads=["LB"], writes=["OML"])
        S.op("act", I("activation", ESINK, ESINK, AF.Exp), reads=["ESINK"], writes=["ESINK"])
        S.op("act", I("activation", LNOML, OML, AF.Ln), reads=["OML"], writes=["LNOML"])

        def prenorm(tiles, grow, col_of, XN, JUNK, HNd=None, junk_keys=("JUNK", "TMPN")):
            for n_, i in enumerate(tiles):
                prenorm_tile(n_, i, grow, col_of(i), XN, JUNK, HNd, junk_keys)

        def prenorm_tile(n_, i, grow, c0, XN, JUNK, HNd=None, junk_keys=("JUNK", "TMPN"), fixed_bank=None):
            HNd = HN if HNd is None else HNd
            if True:
                S.op("act", I("activation", JUNK, X[:, i, :], AF.Square, accum_out=SS[:, i:i + 1]),
                     reads=[("X", i)], writes=list(junk_keys) + [("SS", i)])
                S.op("act", I("activation", LNV[:, i:i + 1], SS[:, i:i + 1], AF.Ln, scale=1.0 / D, bias=EPSC),
                     reads=[("SS", i), "EPSC"], writes=[("LNV", i)])
                S.op("act", I("activation", RSTD[:, i:i + 1], LNV[:, i:i + 1], AF.Exp, scale=-0.5),
                     reads=[("LNV", i)], writes=[("RSTD", i)])
                xn = XN[n_ % 2]
                S.op("dve", I("tensor_scalar", xn, X[:, i, :], RSTD[:, i:i + 1], None, ALU.mult),
                     reads=[("X", i), ("RSTD", i)], writes=[("XN", n_ % 2)])
                b = nbank() if fixed_bank is None else fixed_bank
                pb = bankb(b)
                for c in range(8):
                    S.op("pe", I("transpose", pb[:, c * 128:(c + 1) * 128], xn[:, c * 128:(c + 1) * 128], IDENT),
                         reads=[("XN", n_ % 2), "IDENT"], writes=[("ps", b)])
                g_bc = mkap(GPRE[:, grow, 0:1], [[1, 8], [0, 128]])
                S.op("dve", I("tensor_tensor",
                    HNd[:, :, c0:c0 + 128], pb.rearrange("p (c n) -> p c n", c=8), g_bc, ALU.mult),
                    reads=[("ps", b), "GPRE"], writes=[("HN", c0 // 128)])

        def postnorm(i, pair, GB, TMP, JUNK32):
            ps = pp[pair][:, :]
            S.op("act", I("activation", JUNK32, ps, AF.Square, accum_out=SS2[:, 0:1]),
                 reads=[("ps", 2 * pair), ("ps", 2 * pair + 1)], writes=["JUNK", "TMPN", "SS2"])
            S.op("act", I("activation", SS2[:, 1:2], SS2[:, 0:1], AF.Ln, scale=1.0 / D, bias=EPSC),
                 reads=["SS2", "EPSC"], writes=["SS2b"])
            S.op("act", I("activation", SS2[:, 2:3], SS2[:, 1:2], AF.Exp, scale=-0.5),
                 reads=["SS2b"], writes=["SS2c"])
            S.op("dve", I("scalar_tensor_tensor", TMP, ps, SS2[:, 2:3], GB, ALU.mult, ALU.mult),
                 reads=[("ps", 2 * pair), ("ps", 2 * pair + 1), "SS2c", "GB"], writes=["TMPN"])
            S.op("dve", I("tensor_tensor", X[:, i, :], X[:, i, :], TMP, ALU.add),
                 reads=["TMPN", ("X", i)], writes=[("X", i)])

        def load_gain_bc(GB, l, w):
            src = gains_d[l, w, :]
            src_bc = bass.AP(tensor=src.tensor, offset=src.offset, ap=[[0, 128], [1, D]])
            S.dma("sp", I("dma_start", out=GB, in_=src_bc), writes=["GB"])

        def outproj(CATv, w_d, l, GB, TMP, JUNK32, sb=8):
            for k in range(8):
                load_rowblock(w_d, k * 128, sb + k)
            load_gain_bc(GB, l, 1)
            for i in range(NT):
                pair = pair_ctr[0] % 4
                pair_ctr[0] += 1
                for half in range(2):
                    for k in range(8):
                        S.op("pe", I("matmul",
                            pp[pair][:, half * 512:(half + 1) * 512], lhsT=CATv(k)[:, i * 128:(i + 1) * 128],
                            rhs=WR[:, sb + k, half * 512:(half + 1) * 512], start=(k == 0), stop=(k == 7)),
                            reads=[("CAT", k, i // 4), ("wr", sb + k)], writes=[("ps", 2 * pair + half)])
                postnorm(i, pair, GB, TMP, JUNK32)


        def ffn(l):
            HN2 = V(HN0, BF16, [8, 1024])
            GTb = V(90112, BF16, [NFC, 1024])
            W2b = V(135168, BF16, [NFC, 1024])
            WRF = V(180224, BF16, [8, 1024])
            XNf = [V(196608, BF16, [D]), V(198656, BF16, [D])]
            SG = V(200704, F32, [512])
            TMPf = V(202752, F32, [D])
            JUNKf = V(202752, BF16, [D])
            GBf = V(206848, F32, [D])
            load_gain_bc(GBf, l, 3)
            JUNKs = V(200704, BF16, [D])
            cnt = 0
            prenorm(range(8), 1 + 2 * l, lambda i: i * 128, XNf, JUNKs, HNd=HN2, junk_keys=("SG",))
            for half in range(2):
                tiles = range(8 * half, 8 * half + 8)
                for fc in range(NFC):
                    slots = []
                    for wi, wd in enumerate((w1_d, w3_d)):
                        slot = cnt % 8
                        cnt += 1
                        slots.append(slot)
                        src = wd[l, :, fc * 128:(fc + 1) * 128].rearrange("(c p) n -> p c n", p=128)
                        dst = WRF[:, slot, :].rearrange("p (c n) -> p c n", c=8)
                        S.dma("pool", I("dma_start", out=dst, in_=src), writes=[("wrf", slot)])
                    if half == 0:
                        S.dma("pool", I("dma_start", out=W2b[:, fc, :], in_=w2_d[l, fc * 128:(fc + 1) * 128, :]), writes=[("w2", fc)])
                    for tb in range(2):
                        ba, bb = nbank(), nbank()
                        for wi, bnk in ((0, ba), (1, bb)):
                            wv = WRF[:, slots[wi], :].rearrange("p (c n) -> p c n", c=8)
                            for k in range(8):
                                S.op("pe", I("matmul", bank(bnk), lhsT=wv[:, k, :], rhs=HN2[:, k, tb * 512:(tb + 1) * 512],
                                             start=(k == 0), stop=(k == 7)),
                                     reads=[("wrf", slots[wi])] + [("HN", 4 * tb + j) for j in range(4)], writes=[("ps", bnk)])
                        S.op("act", I("activation", SG, bank(ba), AF.Silu), reads=[("ps", ba)], writes=["SG"])
                        S.op("dve", I("tensor_tensor", GTb[:, fc, tb * 512:(tb + 1) * 512], SG, bank(bb), ALU.mult),
                             reads=["SG", ("ps", bb)], writes=[("gt", fc, tb)])
                for i in tiles:
                    tl = i - 8 * half
                    pair = pair_ctr[0] % 3
                    pair_ctr[0] += 1
                    for hc in range(2):
                        for fc in range(NFC):
                            S.op("pe", I("matmul", pp[pair][:, hc * 512:(hc + 1) * 512], lhsT=GTb[:, fc, tl * 128:(tl + 1) * 128],
                                         rhs=W2b[:, fc, hc * 512:(hc + 1) * 512], start=(fc == 0), stop=(fc == NFC - 1)),
                                 reads=[("gt", fc, tl // 4), ("w2", fc)], writes=[("ps", 2 * pair + hc)])
                    if half == 0:
                        prenorm_tile(tl, 8 + tl, 1 + 2 * l, tl * 128, XNf, JUNKs, HN2, ("SG",), fixed_bank=6 + (tl % 2))
                    postnorm(i, pair, GBf, TMPf, TMPf)
                    if l == 1:
                        S.dma("sp", I("dma_start", out=y_d[i * 128:(i + 1) * 128, :], in_=X[:, i, :]), reads=[("X", i)])
            S.barrier()


        def attn_layer():
            OT = V(122880, BF16, [8, T])
            BH = V(155648, BF16, [3, 8, 128])
            BL = V(155648 + 6144, BF16, [3, 8, 128])
            LOGS = [V(203840, F32, [512]), V(205888, F32, [512])]
            lcnt = [0]
            QT = V(167936, BF16, [4, T])
            KT = V(184320, BF16, [T])
            KT1 = V(199744, BF16, [T])
            VA = V(188416, BF16, [NT, 2, 65])
            PT = [V(192576, BF16, [3, 512]), V(195648, BF16, [3, 512]), V(207936, BF16, [3, 512])]
            OTK = [V(198720, BF16, [256]), V(199232, BF16, [256]), V(211008, BF16, [256])]
            scnt = [0]
            XNa = [V(199744, BF16, [D]), V(201792, BF16, [D])]
            JUNKa = V(203840, BF16, [D])
            GBa = V(155648, F32, [D])
            TMPa = V(167936, F32, [D])
            SCALE = 64.0 ** -0.5
            prenorm(range(NT), 2, lambda i: i * 128, XNa, JUNKa)
            S.op("dve", I("memset", VA[:, :, :, 64:65], 1.0), writes=["VA1"])
            S.barrier()
            S.op("dve", I("memset", KT[64:128, :], 0.0), writes=["KTZ"])
            S.op("dve", I("memset", KT1[0:64, :], 0.0), writes=["KTZ"])
            for pi in range(2):
                for g in range(4):
                    for hl in range(2):
                        hh = 8 * pi + 4 * hl + g
                        src = w_qkv_d[:, 64 * hh:64 * hh + 64].rearrange("(c p) n -> p c n", p=128)
                        dst = WR[:, g, :].rearrange("p (c n) -> p c n", c=8)[:, :, 64 * hl:64 * hl + 64]
                        S.dma("pool", I("dma_start", out=dst, in_=src), writes=[("wr", g)])
                load_colblock(w_qkv_d, 1024 + 128 * pi, 4)
                load_colblock(w_qkv_d, 1280 + 128 * pi, 5)
                S.dma("sp", I("dma_start", out=BH, in_=abias_hi_d[:, :, 8 * pi:8 * pi + 8, :]), writes=["BIAS"])
                S.dma("sp", I("dma_start", out=BL, in_=abias_lo_d[:, :, 8 * pi:8 * pi + 8, :]), writes=["BIAS"])
                for g in range(5):
                    for tb in range(4):
                        b = nbank()
                        for k in range(8):
                            S.op("pe", I("matmul", bank(b), lhsT=wr_cols(g)[:, k, :], rhs=HN[:, k, tb * 512:(tb + 1) * 512],
                                         start=(k == 0), stop=(k == 7)),
                                 reads=[("wr", g)], writes=[("ps", b)])
                        if g < 4:
                            S.op("act", I("copy", QT[:, g, tb * 512:(tb + 1) * 512], bank(b)), reads=[("ps", b)],
                                 writes=[("QT", g, tb)])
                        else:
                            S.op("act", I("copy", KT[0:64, tb * 512:(tb + 1) * 512], bank(b)[0:64, :]), reads=[("ps", b), "KTZ"],
                                 writes=[("KT", tb)])
                            S.op("act", I("copy", KT1[64:128, tb * 512:(tb + 1) * 512], bank(b)[64:128, :]), reads=[("ps", b), "KTZ"],
                                 writes=[("KT1", tb)])
                for t4 in range(4):
                    b = nbank()
                    for tl in range(4):
                        ti = t4 * 4 + tl
                        for k in range(8):
                            S.op("pe", I("matmul", bank(b)[:, tl * 128:(tl + 1) * 128], lhsT=HN[:, k, ti * 128:(ti + 1) * 128],
                                         rhs=wr_cols(5)[:, k, :], start=(k == 0), stop=(k == 7)),
                                 reads=[("wr", 5)], writes=[("ps", b)])
                    S.op("act", I("copy", VA[:, t4 * 4:t4 * 4 + 4, :, 0:64],
                                  bank(b).rearrange("p (a b c) -> p a b c", a=4, b=2)),
                         reads=[("ps", b), "VA1"], writes=[("VA", t4)])
                iters = [(j, kl) for j in range(NT) for kl in range(2)]

                def stA(i):
                    j, kl = iters[i]
                    pb = 64 * kl
                    par = i % 3
                    for o in (-1, 0, 1):
                        if not (0 <= j + o < NT):
                            continue
                        oi = o + 1
                        jj = j + o
                        b = scnt[0] % 4
                        scnt[0] += 1
                        rhs = mkap(QT[:, 0, j * 128:j * 128 + 1], [[T, 4], [1, 128]])
                        ktz = KT if kl == 0 else KT1
                        S.op("pe", I("matmul", bank(b), lhsT=ktz[:, jj * 128:(jj + 1) * 128], rhs=rhs,
                                     start=True, stop=False),
                             reads=[("KT", jj // 4), ("KT1", jj // 4), "KTZ"] + [("QT", g, j // 4) for g in range(4)],
                             writes=[("ps", b)])
                        S.op("pe", I("matmul", bank(b), lhsT=IDENT,
                                     rhs=BH[:, oi, 4 * kl:4 * kl + 4, :].rearrange("p a b -> p (a b)"), start=False, stop=False),
                             reads=["IDENT", "BIAS"], writes=[("ps", b)])
                        S.op("pe", I("matmul", bank(b), lhsT=IDENT,
                                     rhs=BL[:, oi, 4 * kl:4 * kl + 4, :].rearrange("p a b -> p (a b)"), start=False, stop=True),
                             reads=["IDENT", "BIAS"], writes=[("ps", b)])
                        S.op("act", I("activation", PT[par][:, oi, :], bank(b), AF.Exp, scale=SCALE),
                             reads=[("ps", b)], writes=[("PT", par, oi)])

                def stB(i):
                    j, kl = iters[i]
                    kk = 2 * pi + kl
                    par = i % 3
                    offs = [o for o in (-1, 0, 1) if 0 <= j + o < NT]
                    bo = 4 + (i % 2)
                    for g in range(4):
                        for n_, o in enumerate(offs):
                            S.op("pe", I("matmul", bank(bo)[:, g * 65:(g + 1) * 65],
                                         lhsT=PT[par][:, o + 1, g * 128:(g + 1) * 128], rhs=VA[:, j + o, kl, :],
                                         start=(n_ == 0), stop=(n_ == len(offs) - 1)),
                                 reads=[("VA", (j + o) // 4), ("PT", par, o + 1)], writes=[("ps", bo)])
                    den = DEN3[:, 4 * par:4 * par + 4]
                    dps = mkap(bank(bo)[:, 64:65], [[65, 4]])
                    S.op("dve", I("tensor_tensor", den, dps, ESINK[:, 4 * kk:4 * kk + 4], ALU.add),
                         reads=[("ps", bo), "ESINK"], writes=[("DEN", par)])
                    S.op("dve", I("reciprocal", den, den), reads=[("DEN", par)], writes=[("DEN", par)])
                    ov = mkap(bank(bo)[:, 0:1], [[65, 4], [1, 64]])
                    dbc = mkap(den[:, 0:1], [[1, 4], [0, 64]])
                    S.op("dve", I("tensor_tensor", OTK[par].rearrange("p (a b) -> p a b", a=4), ov, dbc, ALU.mult),
                         reads=[("ps", bo), ("DEN", par)], writes=[("OTK", par)])

                def stC(i):
                    j, kl = iters[i]
                    kk = 2 * pi + kl
                    par = i % 3
                    bt = 6 + (i % 2)
                    ptb = bankb(bt)
                    for hp in range(2):
                        S.op("pe", I("transpose", ptb[:, hp * 128:(hp + 1) * 128], OTK[par][:, hp * 128:(hp + 1) * 128], IDENT),
                             reads=[("OTK", par), "IDENT"], writes=[("ps", bt)])
                    S.op("act", I("copy", OT[:, 2 * kk:2 * kk + 2, j * 128:(j + 1) * 128],
                                  ptb[:, 0:256].rearrange("p (a b) -> p a b", a=2)),
                         reads=[("ps", bt)], writes=[("CAT", 2 * kk, j // 4), ("CAT", 2 * kk + 1, j // 4)])

                nI = len(iters)
                for i in range(nI + 2):
                    if i < nI:
                        stA(i)
                    if 0 <= i - 1 < nI:
                        stB(i - 1)
                    if 0 <= i - 2 < nI:
                        stC(i - 2)
            S.barrier()
            outproj(lambda k: OT[:, k, :], w_co_d, 1, GBa, TMPa, TMPa, sb=0)
            S.barrier()

        OF = V(PH0, F32, [4, T])
        CATF = V(PH0 + 32768, BF16, [4, T])
        TM0 = PH0 + 49152

        def catv(k):
            if k < 4:
                return V(PH0 + 8192 * k + 4096, BF16, [T])
            return CATF[:, k - 4, :]
        XN = [V(TM0, BF16, [D]), V(TM0 + 2048, BF16, [D])]
        JUNK = V(TM0 + 4096, BF16, [D])
        prenorm(range(NT), 0, lambda i: i * 128, XN, JUNK)
        if dbg == "hn0":
            dump("hn0", HN, BF16, [8, T])
            dump("rstd", RSTD, F32, [16])
            dump("gpre", GPRE, F32, [4, 8])
            return fin()
        S.barrier()

        AB = V(PH0, BF16, [4, NT, 256])
        TABR = V(TM0, BF16, [4, 2, 512])
        UT = V(TM0 + 8192, BF16, [4, 512])
        for g in range(4):
            load_colblock(w_in_d, 2560 + g * 128, 12 + g)
        for bk in range(4):
            cs = slice(bk * 512, (bk + 1) * 512)
            for g in range(4):
                bu_ = nbank()
                ups = bank(bu_)
                for k in range(8):
                    S.op("pe", I("matmul", ups, lhsT=wr_cols(12 + g)[:, k, :], rhs=HN[:, k, cs],
                                                                     start=(k == 0), stop=(k == 7)),
                         reads=[("wr", 12 + g)], writes=[("ps", bu_)])
                S.op("act", I("copy", UT[:, g, :], ups), reads=[("ps", bu_)], writes=[("UT", g)])
                ba = nbank()
                aps_ = bank(ba)
                ba2 = nbank()
                aps2 = bank(ba2)
                for tl in range(4):
                    tgt = (aps_ if tl < 2 else aps2)[:, (tl % 2) * 256:(tl % 2) * 256 + 256]
                    S.op("pe", I("matmul", tgt, lhsT=UT[:, g, tl * 128:(tl + 1) * 128], rhs=DFTC[:, 0:256], start=True, stop=True),
                         reads=[("UT", g), "DFTC"], writes=[("ps", ba if tl < 2 else ba2)])
                S.op("act", I("copy", AB[:, g, bk * 4:bk * 4 + 2, :].rearrange("p a b -> p (a b)"), aps_),
                     reads=[("ps", ba)], writes=[("AB", g, bk)])
                S.op("act", I("copy", AB[:, g, bk * 4 + 2:bk * 4 + 4, :].rearrange("p a b -> p (a b)"), aps2),
                     reads=[("ps", ba2)], writes=[("AB", g, bk)])
        tabv = dft_t_d.rearrange("(c p) s k -> p c s k", p=128)
        QS = [V(TM0 + 12288, F32, [512]), V(TM0 + 14336, F32, [512])]
        allcat = lambda g: [("CAT", 4 + g, b_) for b_ in range(4)]
        bmid = nbank()
        for g in range(4):
            for c in range(NT):
                S.op("pe", I("matmul", bank(bmid)[:, 2 * g:2 * g + 2], lhsT=AB[:, g, c, 0:128], rhs=DFTC[:, 256:258],
                             start=(c == 0), stop=(c == NT - 1)),
                     reads=[("AB", g, c // 4), "DFTC"], writes=[("ps", bmid)])
        S.op("act", I("copy", CATF[:, :, 1024:1025], mkap(bank(bmid)[:, 0:1], [[2, 4], [1, 1]])),
             reads=[("ps", bmid)], writes=[k_ for g in range(4) for k_ in allcat(g)])
        tcount = 0
        for kb in range(2):
            ks = slice(kb * 512, (kb + 1) * 512)
            for c in range(NT):
                slot = tcount % 4
                tcount += 1
                S.dma("sp", I("dma_start", out=TABR[:, slot, :, :], in_=tabv[:, c, :, ks]),
                      writes=[("tab", slot)])
                for g in range(4):
                    S.op("pe", I("matmul", bank(g), lhsT=AB[:, g, c, 0:128], rhs=TABR[:, slot, 0, :],
                                 start=(c == 0), stop=(c == NT - 1)),
                         reads=[("AB", g, c // 4), ("tab", slot)], writes=[("ps", g)])
                    S.op("pe", I("matmul", bank(4 + g), lhsT=AB[:, g, c, 128:256], rhs=TABR[:, slot, 1, :],
                                 start=(c == 0), stop=(c == NT - 1)),
                         reads=[("AB", g, c // 4), ("tab", slot)], writes=[("ps", 4 + g)])
            for g in range(4):
                qs = QS[g % 2]
                S.op("act", I("copy", qs, bank(4 + g)), reads=[("ps", 4 + g)], writes=[("QS", g % 2)])
                S.op("dve", I("tensor_tensor", CATF[:, g, ks], bank(g), qs, ALU.add),
                     reads=[("ps", g), ("QS", g % 2)], writes=allcat(g))
                if kb == 0:
                    mo = mkap(CATF[:, g, 2047:2048], [[-1, 511]])
                    S.op("dve", I("tensor_tensor", mo, bank(g)[:, 1:512], qs[:, 1:512], ALU.subtract),
                         reads=[("ps", g), ("QS", g % 2)], writes=allcat(g))
                else:
                    mo = mkap(CATF[:, g, 1536:1537], [[-1, 512]])
                    S.op("dve", I("tensor_tensor", mo, bank(g), qs, ALU.subtract),
                         reads=[("ps", g), ("QS", g % 2)], writes=allcat(g))
        if dbg == "fnet":
            dump("catf", CATF, BF16, [4, T])
            dump("ab", AB, BF16, [4, NT, 256])
            return fin()
        S.barrier()

        o_ = [TM0]

        def talloc(dt, shape):
            n = 1
            for s_ in shape:
                n *= s_
            nb = n * (4 if dt == F32 else 2)
            ap = V(o_[0], dt, shape)
            o_[0] += nb
            assert o_[0] <= ARENA_B, o_[0]
            return ap

        T1 = talloc(F32, [512])
        T2 = talloc(F32, [512])
        T3 = talloc(F32, [512])
        T4 = talloc(F32, [512])
        T6 = talloc(F32, [512])
        T7 = talloc(F32, [512])
        QA = talloc(BF16, [512])
        Q2 = talloc(BF16, [512])
        Q3 = talloc(BF16, [512])
        QTS = talloc(BF16, [512])
        KA = talloc(BF16, [512])
        KOUT = talloc(BF16, [512])
        KTET = talloc(BF16, [512])
        KTE = talloc(BF16, [4, 128])
        VT = talloc(BF16, [4, 128])
        PM = [talloc(BF16, [128]) for _ in range(4)]
        SBFL = [talloc(BF16, [128]) for _ in range(3)]

        QSC = 128.0 ** -0.5
        for h in range(4):
            load_colblock(w_in_d, 0 + h * 128, 0 + h)
            load_colblock(w_in_d, 512 + h * 128, 4 + h)
            load_colblock(w_in_d, 1536 + h * 128, 8 + h)

        def rev512(ap2d):
            base = ap2d[:, 511:512]
            return mkap(base, [[-1, 512]])

        def hgrn_sweep(dr):
            fwd = (dr == 0)
            blocks = range(4) if fwd else range(3, -1, -1)
            for h in range(4):
                S.op("dve", I("memset", S32[:, h, :], 0.0), reads=[("SBF", h)], writes=[("S32", h)])
            first_tile = 0 if fwd else NT - 1
            for bk in blocks:
                cs = slice(bk * 512, (bk + 1) * 512)
                for h in range(4):
                    bz, bq, bv = 0, 1, 2
                    zps, qps, vps = bank(bz), bank(bq), bank(bv)
                    for k in range(8):
                        S.op("pe", I("matmul", zps, lhsT=wr_cols(4 + h)[:, k, :], rhs=HN[:, k, cs],
                                                                         start=(k == 0), stop=(k == 7)),
                             reads=[("wr", 4 + h)], writes=[("ps", bz)])
                    for k in range(8):
                        S.op("pe", I("matmul", qps, lhsT=wr_cols(h)[:, k, :], rhs=HN[:, k, cs],
                                                                         start=(k == 0), stop=(k == 7)),
                             reads=[("wr", h)], writes=[("ps", bq)])
                    for tl in range(4):
                        ti = bk * 4 + tl
                        for k in range(8):
                            S.op("pe", I("matmul",
                                vps[:, tl * 128:(tl + 1) * 128], lhsT=HN[:, k, ti * 128:(ti + 1) * 128],
                                rhs=wr_cols(8 + h)[:, k, :], start=(k == 0), stop=(k == 7)),
                                reads=[("wr", 8 + h)], writes=[("ps", bv)])
                    bt = 3
                    ptb = bankb(bt)

                    def chain(hb):
                        sl = slice(256 * hb, 256 * hb + 256)
                        K = lambda n: (n, hb)
                        t1, t2, t3, t4, t6, t7 = (x[:, sl] for x in (T1, T2, T3, T4, T6, T7))
                        qa, q2, q3, qts, ka, kout, ktet = (x[:, sl] for x in (QA, Q2, Q3, QTS, KA, KOUT, KTET))
                        zp, qp = zps[:, sl], qps[:, sl]
                        S.op("act", I("activation", t1, zp, AF.Exp, scale=-1.0), reads=[("ps", bz)], writes=[K("T1")])
                        yield
                        S.op("act", I("activation", t7, t1, AF.Ln, bias=ONEC), reads=[K("T1"), "ONEC"], writes=[K("T7")])
                        yield
                        S.op("act", I("activation", t2, t1, AF.Ln, scale=LB[:, dr, h:h + 1], bias=ONEC),
                             reads=[K("T1"), "ONEC", "LB"], writes=[K("T2")])
                        yield
                        S.op("dve", I("tensor_tensor", t2, t2, t7, ALU.subtract), reads=[K("T2"), K("T7")], writes=[K("T2")])
                        yield
                        S.op("dve", I("scalar_tensor_tensor", t1, zp, -1.0, t7, ALU.mult, ALU.subtract),
                             reads=[("ps", bz), K("T7"), K("T1")], writes=[K("T1")])
                        yield
                        if fwd:
                            g_o, g_i, gt_o = t3, t2, t4
                        else:
                            rv = lambda x: mkap(x[:, 255:256], [[-1, 256]])
                            g_o, g_i, gt_o = rv(t3), rv(t2), rv(t4)
                        S.op("dve", I("tensor_tensor_scan", g_o, RMASK[:, 0, 0:256], g_i, 0.0, ALU.mult, ALU.add),
                             reads=[K("T2"), "RMASK"], writes=[K("T3")])
                        yield
                        S.op("dve", I("tensor_tensor_scan", gt_o, RMASK[:, 1, 0:256], g_i, 0.0, ALU.mult, ALU.add),
                             reads=[K("T2"), "RMASK"], writes=[K("T4")])
                        yield
                        S.op("act", I("activation", t2, t3, AF.Exp), reads=[K("T3")], writes=[K("T2")])
                        yield
                        S.op("act", I("activation", t6, t4, AF.Exp), reads=[K("T4")], writes=[K("T6")])
                        yield
                        S.op("dve", I("tensor_tensor", t3, t1, t3, ALU.subtract), reads=[K("T1"), K("T3"), K("T2")], writes=[K("T3")])
                        yield
                        S.op("act", I("activation", ka, t3, AF.Exp, bias=LNOML[:, dr, h:h + 1]),
                             reads=[K("T3"), "LNOML"], writes=[K("KA")])
                        yield
                        lt_pos = 127 if fwd else 0
                        lt_bc = mkap(t4[:, lt_pos:lt_pos + 1], [[128, 2], [0, 128]])
                        S.op("dve", I("tensor_tensor", t7, t1, t4, ALU.subtract), reads=[K("T1"), K("T4"), K("T7")], writes=[K("T7")])
                        yield
                        S.op("dve", I("tensor_tensor",
                            t7.rearrange("p (a b) -> p a b", a=2), t7.rearrange("p (a b) -> p a b", a=2), lt_bc, ALU.add),
                            reads=[K("T4"), K("T7")], writes=[K("T7")])
                        yield
                        S.op("act", I("activation", ktet, t7, AF.Exp, bias=LNOML[:, dr, h:h + 1]),
                             reads=[K("T7"), "LNOML"], writes=[K("KTET")])
                        yield
                        S.op("dve", I("scalar_tensor_tensor", qa, qp, QSC, t2, ALU.mult, ALU.mult),
                             reads=[("ps", bq), K("T2")], writes=[K("QA")])
                        yield
                        S.op("dve", I("scalar_tensor_tensor", qts, qp, QSC, t6, ALU.mult, ALU.mult),
                             reads=[("ps", bq), K("T6")], writes=[K("QTS")])
                        yield
                        dpos = 31 if fwd else 0
                        if fwd:
                            d1 = mkap(t2[:, dpos:dpos + 1], [[32, 7], [0, 32]])
                            d2 = mkap(t2[:, dpos:dpos + 1], [[32, 6], [0, 32]])
                            q2o, q2i = q2[:, 32:256], qa[:, 32:256]
                            q3o, q3i = q3[:, 64:256], q2[:, 64:256]
                        else:
                            d1 = mkap(t2[:, 32:33], [[32, 7], [0, 32]])
                            d2 = mkap(t2[:, 64:65], [[32, 6], [0, 32]])
                            q2o, q2i = q2[:, 0:224], qa[:, 0:224]
                            q3o, q3i = q3[:, 0:192], q2[:, 0:192]
                        S.op("dve", I("tensor_tensor",
                            q2o.rearrange("p (a b) -> p a b", b=32), q2i.rearrange("p (a b) -> p a b", b=32), d1, ALU.mult),
                            reads=[K("QA"), K("T2")], writes=[K("Q2")])
                        yield
                        S.op("dve", I("tensor_tensor",
                            q3o.rearrange("p (a b) -> p a b", b=32), q3i.rearrange("p (a b) -> p a b", b=32), d2, ALU.mult),
                            reads=[K("Q2"), K("T2")], writes=[K("Q3")])
                        yield
                        down = mkap(t2[:, dpos:dpos + 1], [[32, 8], [0, 32]])
                        S.op("dve", I("tensor_tensor",
                            kout.rearrange("p (a b) -> p a b", b=32), ka.rearrange("p (a b) -> p a b", b=32), down, ALU.mult),
                            reads=[K("KA"), K("T2")], writes=[K("KOUT")])
                        yield
                        for t2l in range(2):
                            tl_ = 2 * hb + t2l
                            S.op("pe", I("transpose", ptb[:, tl_ * 128:(tl_ + 1) * 128], KTET[:, tl_ * 128:(tl_ + 1) * 128], IDENT),
                                 reads=[K("KTET"), "IDENT"], writes=[("ps", bt)])
                        S.op("act", I("copy", KTE[:, 2 * hb:2 * hb + 2, :].rearrange("p a b -> p (a b)"), ptb[:, sl]),
                             reads=[("ps", bt)], writes=[K("KTE")])
                        yield

                    gens = [chain(hb_) for hb_ in ((0, 1) if fwd else (1, 0))]
                    while gens:
                        for g_ in list(gens):
                            try:
                                next(g_)
                            except StopIteration:
                                gens.remove(g_)
                    S.op("act", I("copy", VT.rearrange("p a b -> p (a b)"), vps),
                         reads=[("ps", bv)], writes=["VT"])
                    bs, bo, bu = 4, 5, 6
                    sps, ops_, ups = bank(bs), bank(bo), bank(bu)
                    tls = list(range(4)) if fwd else list(range(3, -1, -1))
                    mk = HMASK[:, 0:128] if fwd else HMASK[:, 128:256]
                    for tl in tls:
                        c0 = tl * 128
                        hb_t = tl // 2
                        sc_ = sps[:, c0:c0 + 128]
                        S.op("pe", I("matmul", sc_, lhsT=KA[:, c0:c0 + 128], rhs=QA[:, c0:c0 + 128], start=True, stop=True),
                             reads=[("KA", hb_t), ("QA", hb_t)], writes=[("ps", bs)])
                        for dd, Qd in ((1, QA), (2, Q2), (3, Q3)):
                            for a in range(4):
                                b_ = a - dd if fwd else a + dd
                                if b_ < 0 or b_ > 3:
                                    continue
                                kw = {}
                                if b_ == 3:
                                    kw["tile_position"] = (0, 96)
                                S.op("pe", I("matmul",
                                    sps[32 * b_:32 * b_ + 32, c0 + 32 * a:c0 + 32 * a + 32],
                                    lhsT=KOUT[:, c0 + 32 * b_:c0 + 32 * b_ + 32], rhs=Qd[:, c0 + 32 * a:c0 + 32 * a + 32],
                                    start=True, stop=True, **kw),
                                    reads=[("KOUT", hb_t), ("QA", hb_t), ("Q2", hb_t), ("Q3", hb_t)], writes=[("ps", bs)])
                    for tl in tls:
                        c0 = tl * 128
                        S.op("dve", I("tensor_tensor", PM[tl], sps[:, c0:c0 + 128], mk, ALU.mult),
                             reads=[("ps", bs), "HMASK"], writes=[("PM", tl)])
                    for tl in tls:
                        c0 = tl * 128
                        S.op("pe", I("matmul", ups[:, c0:c0 + 128], lhsT=KTE[:, tl, :], rhs=VT[:, tl, :], start=True, stop=True,
                                     skip_group_check=True),
                             reads=[("KTE", tl // 2), "VT"], writes=[("ps", bu)])
                    for n_, tl in enumerate(tls):
                        c0 = tl * 128
                        S.op("pe", I("matmul", ops_[:, c0:c0 + 128], lhsT=VT[:, tl, :], rhs=PM[tl], start=(n_ == 0), stop=True,
                                     skip_group_check=True),
                             reads=["VT", ("PM", tl)], writes=[("ps", bo)])
                    for n_, tl in enumerate(tls):
                        c0 = tl * 128
                        ti = bk * 4 + tl
                        if ti != first_tile:
                            sprev = SBF[:, h, :] if n_ == 0 else SBFL[n_ - 1]
                            skey = ("SBF", h) if n_ == 0 else ("SBFL", n_ - 1)
                            S.op("pe", I("matmul", ops_[:, c0:c0 + 128], lhsT=sprev, rhs=QTS[:, c0:c0 + 128], start=False, stop=True,
                                         skip_group_check=True),
                                 reads=[skey, ("QTS", tl // 2)], writes=[("ps", bo)])
                        dtp = (c0 + 127) if fwd else c0
                        S.op("dve", I("scalar_tensor_tensor",
                            S32[:, h, :], S32[:, h, :], T6[:, dtp:dtp + 1], ups[:, c0:c0 + 128], ALU.mult, ALU.add),
                            reads=[("S32", h), ("T6", tl // 2), ("ps", bu)], writes=[("S32", h)])
                        if n_ < 3:
                            S.op("act", I("copy", SBFL[n_], S32[:, h, :]), reads=[("S32", h)], writes=[("SBFL", n_)])
                        else:
                            S.op("act", I("copy", SBF[:, h, :], S32[:, h, :]), reads=[("S32", h)], writes=[("SBF", h)])
                    if fwd:
                        S.op("act", I("copy", OF[:, h, cs], ops_),
                             reads=[("ps", bo)], writes=[("OF", h, bk)])
                    else:
                        S.op("dve", I("tensor_tensor", T1, ops_, OF[:, h, cs], ALU.add),
                             reads=[("ps", bo), ("OF", h, bk)], writes=[("T1", 0), ("T1", 1)])
                        S.op("act", I("activation", Q2, T1, AF.Square), reads=[("T1", 0), ("T1", 1)], writes=[("Q2", 0), ("Q2", 1)])
                        bn = 7
                        nps = bank(bn)
                        S.op("pe", I("matmul", nps, lhsT=ONES, rhs=Q2, start=True, stop=True),
                             reads=["ONES", ("Q2", 0), ("Q2", 1)], writes=[("ps", bn)])
                        S.op("act", I("activation", T2, nps, AF.Ln, scale=1.0 / 128, bias=EPSC),
                             reads=[("ps", bn), "EPSC"], writes=[("T2", 0), ("T2", 1)])
                        S.op("act", I("activation", T2, T2, AF.Exp, scale=-0.5), reads=[("T2", 0), ("T2", 1)], writes=[("T2", 0), ("T2", 1)])
                        S.op("dve", I("scalar_tensor_tensor", T3, T1, OGAIN[:, h:h + 1], T2, ALU.mult, ALU.mult),
                             reads=[("T1", 0), ("T1", 1), ("T2", 0), ("T2", 1), "OGAIN"], writes=[("T3", 0), ("T3", 1)])
                        bg = 3
                        gps = bank(bg)
                        for k in range(8):
                            S.op("pe", I("matmul", gps, lhsT=wr_cols(12 + h)[:, k, :], rhs=HN[:, k, cs],
                                                                             start=(k == 0), stop=(k == 7)),
                                 reads=[("wr", 12 + h)], writes=[("ps", bg)])
                        S.op("act", I("activation", T4, gps, AF.Exp, scale=-1.0),
                             reads=[("ps", bg)], writes=[("T4", 0), ("T4", 1)])
                        S.op("act", I("activation", T4, T4, AF.Ln, bias=ONEC), reads=[("T4", 0), ("T4", 1), "ONEC"], writes=[("T4", 0), ("T4", 1)])
                        S.op("act", I("activation", T4, T4, AF.Exp, scale=-1.0), reads=[("T4", 0), ("T4", 1)], writes=[("T4", 0), ("T4", 1)])
                        S.op("dve", I("tensor_tensor", T4, gps, T4, ALU.mult), reads=[("T4", 0), ("T4", 1), ("ps", bg)], writes=[("T4", 0), ("T4", 1)])
                        S.op("dve", I("tensor_tensor", catv(h)[:, cs], T3, T4, ALU.mult),
                             reads=[("T3", 0), ("T3", 1), ("T4", 0), ("T4", 1)], writes=[("CAT", h, bk)])

        hgrn_sweep(0)
        for h in range(4):
            load_colblock(w_in_d, 1024 + h * 128, 4 + h)
            load_colblock(w_in_d, 2048 + h * 128, 12 + h)
        hgrn_sweep(1)
        S.barrier()

        GB = V(TM0 + 12288, F32, [D])
        TMPN = V(TM0 + 16384, F32, [D])
        JUNK32 = V(TM0 + 20480, BF16, [D])
        outproj(catv, w_abo_d, 0, GB, TMPN, JUNK32)
        S.barrier()

        if stage >= 2:
            ffn(0)
        if stage >= 3:
            attn_layer()
        if stage >= 4:
            ffn(1)
        S.barrier()
        if stage < 4:
            for i in range(NT):
                S.dma("sp", I("dma_start", out=y_d[i * 128:(i + 1) * 128, :], in_=X[:, i, :]),
                      reads=[("X", i)])
        S.emit(nc)
    nc._sched_stats = S.stats
    return nc


_NAMES = ["x", "norm_gains", "ab_w_in", "ab_lb_table", "ab_out_gain", "ab_w_out", "c_w_qkv", "c_sink",
          "c_w_out", "ffn_w1", "ffn_w3", "ffn_w2"]


def make_in_maps(inputs):
    c = _consts()
    f = lambda a: np.ascontiguousarray(np.asarray(a, dtype=np.float32))
    shared = {
        "norm_gains": f(inputs["norm_gains"]),
        "ab_w_in": f(inputs["ab_w_in"])[0],
        "ab_lb_table": f(inputs["ab_lb_table"]),
        "ab_out_gain": f(inputs["ab_out_gain"])[0],
        "ab_w_out": f(inputs["ab_w_out"])[0],
        "c_w_qkv": f(inputs["c_w_qkv"])[0],
        "c_sink": f(inputs["c_sink"]),
        "c_w_out": f(inputs["c_w_out"])[0],
        "ffn_w1": f(inputs["ffn_w1"]),
        "ffn_w3": f(inputs["ffn_w3"]),
        "ffn_w2": f(inputs["ffn_w2"]),
        "k_ident": c["ident"], "k_ident32": c["ident32"], "k_dft_t": c["dft_t"], "k_dft_c": c["dft_c"],
        "k_hmask": c["hmask"], "k_rmask": c["rmask"], "k_abias_hi": c["abias_hi"], "k_abias_lo": c["abias_lo"],
    }
    x = f(inputs["x"])
    return [dict(shared, x=x[b]) for b in range(x.shape[0])]


def kernel(**inputs):
    nc = build()
    in_maps = make_in_maps(inputs)
    res = run_bass_kernel_spmd(nc, in_maps, core_ids=list(range(8)))
    return np.stack([np.asarray(r["y"], dtype=np.float32) for r in res.results], axis=0)
```
